# Optimizing a Trainium2 kernel written in Bass

```python
import jax, jax.numpy as jnp
from jax import lax
import numpy as np

D_MODEL = 1024
BATCH = 8
SEQ = 8192
DEPTH = 2

CHUNK = 64
N_MIXERS = 2
N_CONV_LAYERS = (DEPTH + 1) // 2
N_MLA_LAYERS = DEPTH // 2
D_FF = 2816
FFN_RES_WEIGHT = 0.5
CONV_WIDTH = 31
N_HEADS = 8
QK_NOPE = 128
QK_ROPE = 64
V_HEAD = 128
Q_LORA = 512
KV_LORA = 256
ROPE_THETA = 10000.0
Q_BLOCK = 128
RMS_EPS = 1e-6
POS_OFFSET_MAX = 65536

kernel_name = "hybrid_conformer_conv_mla_macaron"


def rmsnorm(x, g):
    xf = x.astype(jnp.float32)
    y = xf * lax.rsqrt(jnp.mean(xf * xf, axis=-1, keepdims=True) + RMS_EPS) * g.astype(jnp.float32)
    return y.astype(x.dtype)


def swiglu_ffn(h, w1, w3, w2):
    return (jax.nn.silu(h @ w1) * (h @ w3)) @ w2


def conv_module(h, w_pw1, w_dw, g_norm, w_pw2):
    u = h @ w_pw1
    a, b = jnp.split(u, 2, axis=-1)
    u = a * jax.nn.sigmoid(b)
    u = lax.conv_general_dilated(
        u, w_dw[:, None, :].astype(u.dtype), window_strides=(1,),
        padding=[(CONV_WIDTH - 1, 0)],
        dimension_numbers=("NWC", "WIO", "NWC"),
        feature_group_count=D_MODEL)
    u = jax.nn.silu(rmsnorm(u, g_norm))
    return u @ w_pw2


def apply_rope(x, cos, sin):
    half = x.shape[-1] // 2
    xf = x.astype(jnp.float32)
    x1, x2 = xf[..., :half], xf[..., half:]
    out = jnp.concatenate([x1 * cos - x2 * sin, x2 * cos + x1 * sin], axis=-1)
    return out.astype(x.dtype)


def mla(h, positions, w_a, g_q, g_kv, w_uq, w_ukv, w_o):
    a = h @ w_a
    c_q = rmsnorm(a[..., :Q_LORA], g_q)
    c_kv = rmsnorm(a[..., Q_LORA:Q_LORA + KV_LORA], g_kv)
    k_rope = a[..., Q_LORA + KV_LORA:]
    q = jnp.einsum("bsc,chd->bshd", c_q, w_uq)
    q_nope, q_rope = q[..., :QK_NOPE], q[..., QK_NOPE:]
    kv = jnp.einsum("bsc,chd->bshd", c_kv, w_ukv)
    k_nope, v = kv[..., :QK_NOPE], kv[..., QK_NOPE:]

    inv_freq = ROPE_THETA ** (-2.0 * jnp.arange(QK_ROPE // 2, dtype=jnp.float32) / QK_ROPE)
    ang = positions.astype(jnp.float32)[..., None] * inv_freq
    cos, sin = jnp.cos(ang), jnp.sin(ang)
    q_rope = apply_rope(q_rope, cos[:, :, None, :], sin[:, :, None, :])
    k_rope = apply_rope(k_rope, cos, sin)

    seq = h.shape[1]
    scale = (QK_NOPE + QK_ROPE) ** -0.5
    chunk_id = jnp.arange(seq) // CHUNK
    outs = []
    for blk in range(seq // Q_BLOCK):
        q0, q1 = blk * Q_BLOCK, (blk + 1) * Q_BLOCK
        s = (jnp.einsum("bqhd,bkhd->bhqk", q_nope[:, q0:q1], k_nope[:, :q1],
                        preferred_element_type=jnp.float32)
             + jnp.einsum("bqhr,bkr->bhqk", q_rope[:, q0:q1], k_rope[:, :q1],
                          preferred_element_type=jnp.float32)) * scale
        mask = chunk_id[q0:q1, None] >= chunk_id[None, :q1]
        s = jnp.where(mask[None, None], s, -jnp.inf)
        p = jax.nn.softmax(s, axis=-1).astype(v.dtype)
        outs.append(jnp.einsum("bhqk,bkhd->bqhd", p, v[:, :q1]))
    o = jnp.concatenate(outs, axis=1)
    return o.reshape(o.shape[0], seq, N_HEADS * V_HEAD) @ w_o


def setup_inputs(seed: int = 0) -> dict:
    key = jax.random.key(seed)
    ks = jax.random.split(key, 24)
    f32 = jnp.float32

    def w(k, shape, fan_in):
        return jax.random.normal(k, shape, f32) * (fan_in ** -0.5)

    def gain(k, shape):
        return 1.0 + 0.02 * jax.random.normal(k, shape, f32)

    x = jax.random.normal(ks[0], (BATCH, SEQ, D_MODEL), f32)
    offset = jax.random.randint(ks[1], (BATCH, 1), 0, POS_OFFSET_MAX, dtype=jnp.int32)
    positions = offset + jnp.arange(SEQ, dtype=jnp.int32)[None, :]
    return {
        "x": x,
        "positions": positions,
        "ffn_norm1": gain(ks[2], (DEPTH, D_MODEL)),
        "ffn1_w1": w(ks[3], (DEPTH, D_MODEL, D_FF), D_MODEL),
        "ffn1_w3": w(ks[4], (DEPTH, D_MODEL, D_FF), D_MODEL),
        "ffn1_w2": w(ks[5], (DEPTH, D_FF, D_MODEL), D_FF),
        "mix_norm": gain(ks[6], (DEPTH, D_MODEL)),
        "ffn_norm2": gain(ks[7], (DEPTH, D_MODEL)),
        "ffn2_w1": w(ks[8], (DEPTH, D_MODEL, D_FF), D_MODEL),
        "ffn2_w3": w(ks[9], (DEPTH, D_MODEL, D_FF), D_MODEL),
        "ffn2_w2": w(ks[10], (DEPTH, D_FF, D_MODEL), D_FF),
        "conv_w_pw1": w(ks[11], (N_CONV_LAYERS, D_MODEL, 2 * D_MODEL), D_MODEL),
        "conv_w_dw": w(ks[12], (N_CONV_LAYERS, CONV_WIDTH, D_MODEL), CONV_WIDTH),
        "conv_norm": gain(ks[13], (N_CONV_LAYERS, D_MODEL)),
        "conv_w_pw2": w(ks[14], (N_CONV_LAYERS, D_MODEL, D_MODEL), D_MODEL),
        "mla_w_a": w(ks[15], (N_MLA_LAYERS, D_MODEL, Q_LORA + KV_LORA + QK_ROPE), D_MODEL),
        "mla_q_norm": gain(ks[16], (N_MLA_LAYERS, Q_LORA)),
        "mla_kv_norm": gain(ks[17], (N_MLA_LAYERS, KV_LORA)),
        "mla_w_uq": w(ks[18], (N_MLA_LAYERS, Q_LORA, N_HEADS, QK_NOPE + QK_ROPE), Q_LORA),
        "mla_w_ukv": w(ks[19], (N_MLA_LAYERS, KV_LORA, N_HEADS, QK_NOPE + V_HEAD), KV_LORA),
        "mla_w_o": w(ks[20], (N_MLA_LAYERS, N_HEADS * V_HEAD, D_MODEL), N_HEADS * V_HEAD),
        "final_norm": gain(ks[21], (D_MODEL,)),
    }


def reference(x, positions, ffn_norm1, ffn1_w1, ffn1_w3, ffn1_w2, mix_norm, ffn_norm2,
              ffn2_w1, ffn2_w3, ffn2_w2, conv_w_pw1, conv_w_dw, conv_norm, conv_w_pw2,
              mla_w_a, mla_q_norm, mla_kv_norm, mla_w_uq, mla_w_ukv, mla_w_o, final_norm):
    h = x
    for i in range(DEPTH):
        h = h + FFN_RES_WEIGHT * swiglu_ffn(rmsnorm(h, ffn_norm1[i]), ffn1_w1[i], ffn1_w3[i], ffn1_w2[i])
        m = rmsnorm(h, mix_norm[i])
        j = i // N_MIXERS
        if i % N_MIXERS == 0:
            h = h + conv_module(m, conv_w_pw1[j], conv_w_dw[j], conv_norm[j], conv_w_pw2[j])
        else:
            h = h + mla(m, positions, mla_w_a[j], mla_q_norm[j], mla_kv_norm[j],
                        mla_w_uq[j], mla_w_ukv[j], mla_w_o[j])
        h = h + FFN_RES_WEIGHT * swiglu_ffn(rmsnorm(h, ffn_norm2[i]), ffn2_w1[i], ffn2_w3[i], ffn2_w2[i])
    return rmsnorm(h, final_norm)
```

```python
import numpy as np
from contextlib import ExitStack
import concourse.bass as bass
import concourse.mybir as mybir
from concourse.bass_utils import run_bass_kernel_spmd

F32 = mybir.dt.float32
BF16 = mybir.dt.bfloat16
I32 = mybir.dt.int32
AF = mybir.ActivationFunctionType
ALU = mybir.AluOpType

D = 1024
DC = 8
DFF = 2816
NF = 22
CW = 31
NH = 8
QL, KVL, ROPE = 512, 256, 64
EPS = 1e-6
SCALE = float((128 + 64) ** -0.5)
NT = 2
T = 512 * NT
TB = T // 128
SLABW = 4096
NSLOT = 4
HALO = 32
DEBUG = False


def _stream_layout():
    sl = []
    for L in range(2):
        for nm in ("f1", "f2"):
            pass
    def ffn(tag):
        return [(f"{tag}.w13.{i}", 4096) for i in range(11)] + [(f"{tag}.w2.{k}", NF * 128) for k in range(8)]
    sl += ffn("L0f1")
    sl += [(f"pw1.{s}", 4096) for s in range(4)]
    sl += [(f"dw.{i}", CW * 128) for i in range(8)]
    sl += [(f"pw2.{s}", 4096) for s in range(2)]
    sl += ffn("L0f2")
    sl += ffn("L1f1")
    sl += [("wa.0", 4096), ("wa.1", 4096), ("wukv", 4096)]
    sl += [(f"wuq.{s}", 3072) for s in range(4)]
    sl += [(f"wo.{s}", 4096) for s in range(2)]
    sl += ffn("L1f2")
    offs = {}
    o = 0
    for n, w in sl:
        offs[n] = (o, w)
        o += w
    return sl, offs, o


STREAM, SOFF, TOTW = _stream_layout()

COL = {}
_c = 0
for _n, _w in [("fn1", 16), ("mix", 16), ("fn2", 16), ("cnorm", 8), ("final", 8), ("qn", 4), ("kvn", 2),
               ("invf", 1), ("sgn", 1)]:
    COL[_n] = _c
    _c += _w
NCOL = _c


def _blk(W, m, mw=128):
    K = W.shape[0]
    kc = K // 128
    return np.ascontiguousarray(W[:, m * mw:(m + 1) * mw].reshape(kc, 128, mw).transpose(1, 0, 2)).reshape(128, kc * mw)


def _pad_cols(Wk64):
    out = np.zeros((Wk64.shape[0], 128), np.float32)
    out[:, :64] = Wk64
    return out


def pack_weights(p):
    wall = np.zeros((128, TOTW), np.float32)

    def put(name, arr):
        o, w = SOFF[name]
        assert arr.shape == (128, w), (name, arr.shape, w)
        wall[:, o:o + w] = arr

    def ffn(tag, w1, w3, w2):
        for i in range(11):
            put(f"{tag}.w13.{i}", np.concatenate(
                [_blk(w1, 2 * i), _blk(w1, 2 * i + 1), _blk(w3, 2 * i), _blk(w3, 2 * i + 1)], axis=1))
        for k in range(8):
            put(f"{tag}.w2.{k}", _blk(w2, k))

    ffn("L0f1", p["ffn1_w1"][0], p["ffn1_w3"][0], p["ffn1_w2"][0])
    ffn("L0f2", p["ffn2_w1"][0], p["ffn2_w3"][0], p["ffn2_w2"][0])
    ffn("L1f1", p["ffn1_w1"][1], p["ffn1_w3"][1], p["ffn1_w2"][1])
    ffn("L1f2", p["ffn2_w1"][1], p["ffn2_w3"][1], p["ffn2_w2"][1])
    pw1 = p["conv_w_pw1"][0]
    for s in range(4):
        put(f"pw1.{s}", np.concatenate(
            [_blk(pw1, 2 * s), _blk(pw1, 8 + 2 * s), _blk(pw1, 2 * s + 1), _blk(pw1, 8 + 2 * s + 1)], axis=1))
    wdw = p["conv_w_dw"][0]
    for i in range(8):
        dg = np.zeros((128, CW, 128), np.float32)
        idx = np.arange(128)
        dg[idx, :, idx] = wdw[:, i * 128:(i + 1) * 128].T
        put(f"dw.{i}", dg.reshape(128, CW * 128))
    pw2 = p["conv_w_pw2"][0]
    for s in range(2):
        put(f"pw2.{s}", np.concatenate([_blk(pw2, 4 * s + j) for j in range(4)], axis=1))
    wa = p["mla_w_a"][0]
    put("wa.0", np.concatenate([_blk(wa, j) for j in range(4)], axis=1))
    kr = wa[:, 768:832]
    krp = np.concatenate([kr[:, 32:], kr[:, :32]], axis=1)
    put("wa.1", np.concatenate([_blk(wa, 4), _blk(wa, 5), _blk(_pad_cols(kr), 0), _blk(_pad_cols(krp), 0)], axis=1))
    ukv = p["mla_w_ukv"][0]
    uk = np.concatenate([_blk(np.ascontiguousarray(ukv[:, h, :128]), 0) for h in range(NH)], axis=1)
    uv = np.ascontiguousarray(ukv[:, :, 128:]).reshape(256, NH * 128)
    uvl = np.ascontiguousarray(uv.reshape(2, 128, 1024).transpose(1, 0, 2)).reshape(128, 2048)
    put("wukv", np.concatenate([uk, uvl], axis=1))
    uq = p["mla_w_uq"][0]
    for s in range(4):
        parts = []
        for h in (2 * s, 2 * s + 1):
            qn = np.ascontiguousarray(uq[:, h, :128])
            qr = uq[:, h, 128:]
            qrp = np.concatenate([qr[:, 32:], qr[:, :32]], axis=1)
            parts += [_blk(qn, 0), _blk(_pad_cols(qr), 0), _blk(_pad_cols(qrp), 0)]
        put(f"wuq.{s}", np.concatenate(parts, axis=1))
    wo = p["mla_w_o"][0]
    for s in range(2):
        put(f"wo.{s}", np.concatenate([_blk(wo, 4 * s + j) for j in range(4)], axis=1))
    return wall


def pack_cols(p):
    cols = np.zeros((128, NCOL), np.float32)

    def colv(v):
        return np.ascontiguousarray(np.asarray(v, np.float32).reshape(-1, 128).T)

    for L in range(2):
        cols[:, COL["fn1"] + 8 * L:COL["fn1"] + 8 * L + 8] = colv(p["ffn_norm1"][L])
        cols[:, COL["mix"] + 8 * L:COL["mix"] + 8 * L + 8] = colv(p["mix_norm"][L])
        cols[:, COL["fn2"] + 8 * L:COL["fn2"] + 8 * L + 8] = colv(p["ffn_norm2"][L])
    cols[:, COL["cnorm"]:COL["cnorm"] + 8] = colv(p["conv_norm"][0])
    cols[:, COL["final"]:COL["final"] + 8] = colv(p["final_norm"])
    cols[:, COL["qn"]:COL["qn"] + 4] = colv(p["mla_q_norm"][0])
    cols[:, COL["kvn"]:COL["kvn"] + 2] = colv(p["mla_kv_norm"][0])
    import jax
    import jax.numpy as jnp
    with jax.default_device(jax.devices("cpu")[0]):
        invf = np.asarray(10000.0 ** (-2.0 * jnp.arange(32, dtype=jnp.float32) / 64))
    cols[:, COL["invf"]] = np.tile(invf, 4)
    sg = np.ones(128, np.float32)
    sg[0:32] = -1.0
    sg[64:96] = -1.0
    cols[:, COL["sgn"]] = sg
    return cols


class Buf:
    __slots__ = ("name", "w", "r")

    def __init__(self, name, legacy=None):
        self.name = name
        self.w = {}
        self.r = dict(legacy) if legacy else {}


class Q:
    def __init__(self, name, kind):
        self.name = name
        self.kind = kind
        self.ops = []
        self.count = 0
        self.waited = {}
        self.semkey = "e_" + name


class Prog:
    def __init__(self):
        self.q = {"pe": Q("pe", "pe"), "act": Q("act", "c"), "dve": Q("dve", "c"),
                  "sp": Q("sp", "dma"), "pool": Q("pool", "dma")}
        self.semcnt = {}
        self.semkeys = set(q.semkey for q in self.q.values() if q.kind != "dma")

    def op(self, qn, fn, reads=(), writes=(), sig=True, dsem=None, partial=False):
        E = self.q[qn]
        deps = {}

        def add(d):
            for k, v in d.items():
                if deps.get(k, 0) < v:
                    deps[k] = v
        for b in reads:
            add(b.w)
        for b in writes:
            add(b.w)
            add(b.r)
        waits = []
        for k, v in deps.items():
            if E.kind == "pe" and k == E.semkey:
                continue
            if E.waited.get(k, 0) < v:
                waits.append((k, v))
                E.waited[k] = v
        if E.kind == "dma":
            assert dsem is not None
            c = self.semcnt.get(dsem, 0) + 16
            self.semcnt[dsem] = c
            self.semkeys.add(dsem)
            ev = (dsem, c)
            inc = (dsem, 16)
        elif sig:
            E.count += 1
            ev = (E.semkey, E.count)
            inc = (E.semkey, 1)
        else:
            ev = (E.semkey, E.count + 1)
            inc = None
        assert ev[1] < 60000, "semaphore value too large"
        E.ops.append((waits, fn, inc))
        for b in reads:
            if b.r.get(ev[0], 0) < ev[1]:
                b.r[ev[0]] = ev[1]
        for b in writes:
            if partial:
                if b.w.get(ev[0], 0) < ev[1]:
                    b.w[ev[0]] = ev[1]
            else:
                b.w = {ev[0]: ev[1]}
            b.r = {}
        return ev


def legacy_of(bufs):
    leg = {}
    for b in bufs:
        for d in (b.w, b.r):
            for k, v in d.items():
                if leg.get(k, 0) < v:
                    leg[k] = v
    return leg


def build_program(SEQ):
    NTILES = SEQ // T
    NBLK = SEQ // 128
    nc = bass.Bass("TRN2", target_bir_lowering=False)
    x_d = nc.dram_tensor("x", [SEQ, D], F32, kind="ExternalInput").ap()
    pos_d = nc.dram_tensor("pos", [1, SEQ], I32, kind="ExternalInput").ap()
    wall_d = nc.dram_tensor("wall", [128, TOTW], F32, kind="ExternalInput").ap()
    cols_d = nc.dram_tensor("cols", [128, NCOL], F32, kind="ExternalInput").ap()
    ident_d = nc.dram_tensor("ident", [128, 128], F32, kind="ExternalInput").ap()
    out_d = nc.dram_tensor("out", [SEQ, D], F32, kind="ExternalOutput").ap()
    wbf_d = nc.dram_tensor("wbf", [128, TOTW], BF16, kind="Internal").ap()
    kc_d = nc.dram_tensor("kc", [128, NH, SEQ], BF16, kind="Internal").ap()
    krc_d = nc.dram_tensor("krc", [128, SEQ], BF16, kind="Internal").ap()
    vc_d = nc.dram_tensor("vc", [128, NH, NBLK, 128], BF16, kind="Internal").ap()
    dbg_d = nc.dram_tensor("dbg", [128, DC * T], BF16, kind="Internal").ap() if DEBUG else None
    dbg2_d = nc.dram_tensor("dbg2", [128, 6 * T], BF16, kind="Internal").ap() if DEBUG else None

    P = Prog()
    es = ExitStack()

    def sb(name, shape, dt):
        return es.enter_context(nc.sbuf_tensor(name, shape, dt))

    H = sb("H", [128, DC, T], F32)
    XN = sb("XN", [128, DC, T], BF16)
    ARW = 28672
    AR = sb("AR", [128, ARW], BF16)
    WR = [sb(f"WR{i}", [128, SLABW], BF16) for i in range(NSLOT)]
    IOB = [sb(f"IOB{i}", [128, D], F32) for i in range(4)]
    COLS = sb("COLS", [128, NCOL], F32)
    IDENT = sb("IDENT", [128, 128], F32)
    ONES = sb("ONES", [128, 128], BF16)
    SQ = [sb(f"SQ{i}", [128, 512], BF16) for i in range(4)]
    SG = [sb(f"SG{i}", [128, 512], F32) for i in range(2)]
    LNT = [sb(f"LNT{i}", [128, 512], F32) for i in range(2)]
    RS = [sb(f"RS{i}", [128, 512], F32) for i in range(2)]
    UH = sb("UH", [128, DC, HALO], BF16)
    TRG = [sb(f"TRG{i}", [128, 512], F32) for i in range(3)]
    POSI = sb("POSI", [128, 512], I32)
    COS = sb("COS", [128, T], F32)
    SIN = sb("SIN", [128, T], F32)
    BK = [es.enter_context(nc.psum_tensor(f"BK{i}", [128, 512], F32)) for i in range(8)]

    bH = [[Buf(f"H{c}.{s}") for s in range(NT)] for c in range(DC)]
    bXN = [[Buf(f"XN{c}.{s}") for s in range(NT)] for c in range(DC)]
    bWR = [Buf(f"WR{i}") for i in range(NSLOT)]
    bIOB = [Buf(f"IOB{i}") for i in range(4)]
    bBK = [Buf(f"BK{i}") for i in range(8)]
    bCOLS, bIDENT, bONES = Buf("cols"), Buf("ident"), Buf("ones")
    bSQ = [Buf(f"sq{i}") for i in range(4)]
    bSG = [Buf("sg0"), Buf("sg1")]
    bLNT = [Buf("lnt0"), Buf("lnt1")]
    bRS = [Buf("rs0"), Buf("rs1")]
    bUH = Buf("uh")
    bTRG = [Buf(f"trg{i}") for i in range(3)]
    bPOSI = Buf("posi")
    bCOS = [Buf(f"cos{s}") for s in range(NT)]
    bSIN = [Buf(f"sin{s}") for s in range(NT)]
    bWBF = Buf("wbf")
    bKD = [Buf(f"kd{j}") for j in range(NTILES)]
    bRD = [Buf(f"rd{j}") for j in range(NTILES)]
    bVD = [Buf(f"vd{j}") for j in range(NTILES)]

    def col(name, j=0):
        c = COL[name] + j
        return COLS[:, c:c + 1]

    P.op("sp", lambda e: e.dma_start(out=COLS[:], in_=cols_d[:]), writes=[bCOLS], dsem="setup0")
    P.op("sp", lambda e: e.dma_start(out=IDENT[:], in_=ident_d[:]), writes=[bIDENT], dsem="setup1")
    P.op("dve", lambda e: e.memset(ONES[:], 1.0), writes=[bONES])
    P.op("dve", lambda e: e.memset(UH[:], 0.0), writes=[bUH])
    NPRE = 32
    step = (TOTW + NPRE - 1) // NPRE
    step = (step + 1023) // 1024 * 1024
    pre_chunks = []
    o = 0
    while o < TOTW:
        w = min(step, TOTW - o)
        pre_chunks.append((o, o + w, Buf(f"wbf{len(pre_chunks)}")))
        o += w
    pre_state = {"issued": 0}

    def issue_prepass_upto(k):
        while pre_state["issued"] < min(k, len(pre_chunks)):
            n_ = pre_state["issued"]
            c0, c1, b = pre_chunks[n_]
            P.op("pool", (lambda c0=c0, c1=c1: lambda e: e.dma_start(out=wbf_d[:, c0:c1], in_=wall_d[:, c0:c1],
                                                                      max_dma_last_dim=4096))(),
                 writes=[b], dsem=f"pre{n_}")
            pre_state["issued"] += 1

    wstate = {"n": 0, "loaded": 0}
    total_slabs = NTILES * len(STREAM)

    def issue_load(m):
        name, w = STREAM[m % len(STREAM)]
        off = SOFF[name][0]
        slot = m % NSLOT
        need = [n_ for n_, (c0, c1, b) in enumerate(pre_chunks) if c0 < off + w and off < c1]
        issue_prepass_upto(max(need) + 3)
        rd = [pre_chunks[n_][2] for n_ in need]
        P.op("sp", lambda e: e.dma_start(out=WR[slot][:, 0:w], in_=wbf_d[:, off:off + w]),
             reads=rd, writes=[bWR[slot]], dsem=f"wr{slot}")

    def next_slab(expect, hold=0):
        n = wstate["n"]
        name, w = STREAM[n % len(STREAM)]
        assert name.endswith(expect) or expect in name, (name, expect)
        while wstate["loaded"] < min(total_slabs, max(n + 1, n + NSLOT - hold)):
            issue_load(wstate["loaded"])
            wstate["loaded"] += 1
        wstate["n"] = n + 1
        return WR[n % NSLOT], bWR[n % NSLOT]

    rr = {"sq": 0, "sg": 0, "bank": 0}
    arena = {"bufs": []}

    def mm(out, lhsT, rhs, start, stop, reads, wbuf, sig):
        P.op("pe", lambda e: e.matmul(out, lhsT, rhs, start=start, stop=stop), reads=reads, writes=[wbuf], sig=sig)

    def rstd_to(bank, Dn, dst_ap, dst_buf, li=0):
        P.op("act", lambda e: e.activation(out=LNT[li][:], in_=BK[bank][:], func=AF.Ln, scale=1.0 / Dn, bias=EPS),
             reads=[bBK[bank]], writes=[bLNT[li]])
        P.op("act", lambda e: e.activation(out=dst_ap, in_=LNT[li][:], func=AF.Exp, scale=-0.5),
             reads=[bLNT[li]], writes=[dst_buf])

    def sumsq_sbuf(src_ap_fn, src_bufs, nchunks, bank, st):
        for c in range(nchunks):
            k = rr["sq"] % 4
            rr["sq"] += 1
            src = src_ap_fn(c)
            P.op("act", (lambda src=src, k=k: lambda e: e.activation(out=SQ[k][:], in_=src, func=AF.Square))(),
                 reads=[src_bufs[c]], writes=[bSQ[k]])
            mm(BK[bank][:], ONES[:], SQ[k][:], c == 0, c == nchunks - 1, [bONES, bSQ[k]], bBK[bank], True)

    nstate = {"pend": [], "cnt": [0] * NT, "target": None, "done": [False] * NT}

    def xn_target(gname, gofs):
        return (gname, gofs, lambda c, sl: XN[:, c, sl], lambda c, st: bXN[c][st], False)

    def norm_begin(target):
        assert not nstate["pend"] and nstate["cnt"] == [0] * NT
        nstate["target"] = target

    def _norm_complete(st):
        rstd_to(7 - st, D, BK[7 - st][:], bBK[7 - st], li=st)
        nstate["done"][st] = True
        tgt = nstate["target"]
        if tgt is None:
            return
        gname, gofs, dst_fn, dst_bufs_fn, dst_is_H = tgt
        sl = slice(st * 512, (st + 1) * 512)
        for c in range(DC):
            dst = dst_fn(c, sl)
            P.op("dve", (lambda c=c, sl=sl, dst=dst, st=st, gname=gname, gofs=gofs: lambda e: e.scalar_tensor_tensor(
                out=dst, in0=H[:, c, sl], scalar=col(gname, gofs + c), in1=BK[7 - st][:],
                op0=ALU.mult, op1=ALU.mult))(),
                reads=[bH[c][st], bBK[7 - st], bCOLS] if not dst_is_H else [bBK[7 - st], bCOLS],
                writes=[dst_bufs_fn(c, st)])

    def _norm_flush_one():
        k, st = nstate["pend"].pop(0)
        first = nstate["cnt"][st] == 0
        nstate["cnt"][st] += 1
        last = nstate["cnt"][st] == DC
        mm(BK[7 - st][:], ONES[:], SQ[k][:], first, last, [bONES, bSQ[k]], bBK[7 - st], True)
        if last:
            _norm_complete(st)

    def norm_flush_pending():
        while nstate["pend"]:
            _norm_flush_one()

    def norm_push_src(src, src_buf, st):
        k = rr["sq"] % 4
        rr["sq"] += 1
        P.op("act", (lambda src=src, k=k: lambda e: e.activation(out=SQ[k][:], in_=src, func=AF.Square))(),
             reads=[src_buf], writes=[bSQ[k]])
        nstate["pend"].append((k, st))
        if len(nstate["pend"]) > 2:
            _norm_flush_one()

    def norm_push(c, st):
        norm_push_src(H[:, c, st * 512:(st + 1) * 512], bH[c][st], st)

    def norm_stats_finish():
        norm_flush_pending()
        assert nstate["cnt"] == [DC] * NT and all(nstate["done"]), (nstate["cnt"], nstate["done"])
        nstate["cnt"] = [0] * NT
        nstate["done"] = [False] * NT
        nstate["target"] = None

    def to_XN(gname, gofs):
        assert nstate["target"] is not None and nstate["target"][:2] == (gname, gofs), (nstate["target"], gname, gofs)
        norm_stats_finish()

    def ffn(tag, extra=None):
        G = AR[:, 0:NF * T].rearrange("p (f t) -> p f t", t=T)
        leg = legacy_of(arena["bufs"])
        bG = [[Buf(f"G{f}.{s}", leg) for s in range(NT)] for f in range(NF)]
        arena["bufs"] = [b for row in bG for b in row]
        n = 0
        for i in range(11):
            W, bW = next_slab(f"{tag}.w13.{i}")
            for st in range(NT):
                for j in range(2):
                    f = 2 * i + j
                    sl = slice(st * 512, (st + 1) * 512)
                    b1, b3 = (n % 2), 2 + (n % 2)
                    n += 1
                    for c in range(DC):
                        mm(BK[b1][:], W[:, (j * 8 + c) * 128:(j * 8 + c + 1) * 128], XN[:, c, sl], c == 0, c == DC - 1,
                           [bW, bXN[c][st]], bBK[b1], c == DC - 1)
                        mm(BK[b3][:], W[:, 2048 + (j * 8 + c) * 128:2048 + (j * 8 + c + 1) * 128], XN[:, c, sl],
                           c == 0, c == DC - 1, [bW, bXN[c][st]], bBK[b3], c == DC - 1)
                    k = rr["sg"] % 2
                    rr["sg"] += 1
                    P.op("act", (lambda b1=b1, k=k: lambda e: e.activation(out=SG[k][:], in_=BK[b1][:], func=AF.Silu))(),
                         reads=[bBK[b1]], writes=[bSG[k]])
                    P.op("dve", (lambda f=f, sl=sl, b3=b3, k=k: lambda e: e.tensor_tensor(
                        out=G[:, f, sl], in0=BK[b3][:], in1=SG[k][:], op=ALU.mult))(),
                        reads=[bBK[b3], bSG[k]], writes=[bG[f][st]])
                    if extra:
                        extra.pop(0)()
        while extra:
            extra.pop(0)()
        for kb in range(8):
            W, bW = next_slab(f"{tag}.w2.{kb}")
            for st in range(NT):
                sl = slice(st * 512, (st + 1) * 512)
                by = 4 + (n % 2)
                n += 1
                for f in range(NF):
                    mm(BK[by][:], W[:, f * 128:(f + 1) * 128], G[:, f, sl], f == 0, f == NF - 1,
                       [bW, bG[f][st]], bBK[by], f == NF - 1)
                    if f == 10:
                        norm_flush_pending()
                P.op("dve", (lambda kb=kb, sl=sl, by=by: lambda e: e.scalar_tensor_tensor(
                    out=H[:, kb, sl], in0=BK[by][:], scalar=0.5, in1=H[:, kb, sl], op0=ALU.mult, op1=ALU.add))(),
                    reads=[bBK[by]], writes=[bH[kb][st]])
                norm_push(kb, st)


    MAGIC = 12582912.0
    INV2PI = float(np.float32(1.0 / (2.0 * np.pi)))
    C1 = 6.28125
    C2 = float(2.0 * np.pi - 6.28125)
    PI_LO = 3.1415925
    HPI = float(np.pi / 2)
    kvctr = {"kst": 0, "vst": 0, "stage": 0, "pt": 0, "sbank": 0}

    def trig_tables(i):
        t0 = i * T
        thunks = []

        class _Rec:
            def op(self, *a, **k):
                thunks.append(lambda: P.op(*a, **k))
        PR = _Rec()
        for q in range(NT):
            sl = slice(q * 512, (q + 1) * 512)
            PR.op("pool", (lambda q=q: lambda e: e.dma_start(
                out=POSI[:], in_=pos_d[0:1, t0 + q * 512:t0 + (q + 1) * 512].partition_broadcast(128)))(),
                writes=[bPOSI], dsem="posi")
            PR.op("dve", lambda e: e.tensor_scalar(out=TRG[0][:], in0=POSI[:], scalar1=col("invf"), scalar2=None,
                                                  op0=ALU.mult), reads=[bPOSI, bCOLS], writes=[bTRG[0]])
            PR.op("dve", lambda e: e.tensor_scalar(out=TRG[1][:], in0=TRG[0][:], scalar1=INV2PI, scalar2=MAGIC,
                                                  op0=ALU.mult, op1=ALU.add), reads=[bTRG[0]], writes=[bTRG[1]])
            PR.op("dve", lambda e: e.tensor_scalar(out=TRG[1][:], in0=TRG[1][:], scalar1=MAGIC, scalar2=None,
                                                  op0=ALU.subtract), reads=[bTRG[1]], writes=[bTRG[1]])
            for cc in (C1, C2):
                PR.op("dve", (lambda cc=cc: lambda e: e.scalar_tensor_tensor(
                    out=TRG[0][:], in0=TRG[1][:], scalar=-cc, in1=TRG[0][:], op0=ALU.mult, op1=ALU.add))(),
                    reads=[bTRG[1]], writes=[bTRG[0]])
            PR.op("dve", lambda e: e.tensor_scalar(out=TRG[0][:], in0=TRG[0][:], scalar1=-PI_LO, scalar2=PI_LO,
                                                  op0=ALU.max, op1=ALU.min), reads=[], writes=[bTRG[0]])
            PR.op("act", (lambda sl=sl: lambda e: e.activation(out=SIN[:, sl], in_=TRG[0][:], func=AF.Sin,
                                                              scale=col("sgn")))(),
                 reads=[bTRG[0], bCOLS], writes=[bSIN[q]])
            PR.op("dve", lambda e: e.tensor_scalar(out=TRG[2][:], in0=TRG[0][:], scalar1=HPI, scalar2=-2.0 * np.pi,
                                                  op0=ALU.is_gt, op1=ALU.mult), reads=[bTRG[0]], writes=[bTRG[2]])
            PR.op("dve", lambda e: e.scalar_tensor_tensor(out=TRG[1][:], in0=TRG[0][:], scalar=HPI, in1=TRG[2][:],
                                                         op0=ALU.add, op1=ALU.add),
                 reads=[bTRG[0], bTRG[2]], writes=[bTRG[1]])
            PR.op("dve", lambda e: e.tensor_scalar(out=TRG[1][:], in0=TRG[1][:], scalar1=-PI_LO, scalar2=PI_LO,
                                                  op0=ALU.max, op1=ALU.min), reads=[], writes=[bTRG[1]])
            PR.op("act", (lambda sl=sl: lambda e: e.activation(out=COS[:, sl], in_=TRG[1][:], func=AF.Sin))(),
                 reads=[bTRG[1]], writes=[bCOS[q]])
        return thunks

    def mla(i):
        t0 = i * T
        o_ = 0

        def carve(n):
            nonlocal o_
            a = AR[:, o_:o_ + n]
            o_ += n
            return a
        CQ = carve(4 * T).rearrange("p (c t) -> p c t", t=T)
        CKV = carve(2 * T).rearrange("p (c t) -> p c t", t=T)
        KST = [carve(T) for _ in range(4)]
        KRS = carve(T)
        VST = [carve(1024) for _ in range(4)]
        QN = [carve(T) for _ in range(2)]
        QR = [carve(T) for _ in range(2)]
        STK = [carve(T) for _ in range(2)]
        STR = [carve(T) for _ in range(2)]
        STV = [carve(T) for _ in range(2)]
        PT = [carve(512) for _ in range(4)]
        assert o_ <= ARW, o_
        leg = legacy_of(arena["bufs"])
        mk = lambda n: Buf(n, leg)
        bCQ = [[mk("cq") for s in range(NT)] for c in range(4)]
        bCKV = [[mk("ckv") for s in range(NT)] for c in range(2)]
        bKST = [mk(f"kst{k}") for k in range(4)]
        bKRS = [mk("krs") for s in range(NT)]
        bVST = [mk(f"vst{k}") for k in range(4)]
        bQN = [[mk("qn") for s in range(NT)] for _ in range(2)]
        bQR = [[mk("qr") for s in range(NT)] for _ in range(2)]
        bSTK = [mk("stk0"), mk("stk1")]
        bSTR = [mk("str0"), mk("str1")]
        bSTV = [mk("stv0"), mk("stv1")]
        bPT = [mk(f"pt{k}") for k in range(4)]
        arena["bufs"] = ([b for r_ in bCQ + bCKV + bQN + bQR for b in r_] + bKST + bKRS + bVST + bSTK + bSTR + bSTV
                         + bPT)

        def rope_combine(bank_x, bank_p, q, dst_ap, dst_buf):
            sl = slice(q * 512, (q + 1) * 512)
            P.op("dve", lambda e: e.tensor_tensor(out=TRG[0][:], in0=BK[bank_x][:], in1=COS[:, sl], op=ALU.mult),
                 reads=[bBK[bank_x], bCOS[q]], writes=[bTRG[0]])
            P.op("dve", lambda e: e.tensor_tensor(out=TRG[1][:], in0=BK[bank_p][:], in1=SIN[:, sl], op=ALU.mult),
                 reads=[bBK[bank_p], bSIN[q]], writes=[bTRG[1]])
            P.op("dve", lambda e: e.tensor_tensor(out=dst_ap, in0=TRG[0][:], in1=TRG[1][:], op=ALU.add),
                 reads=[bTRG[0], bTRG[1]], writes=[dst_buf])

        WA0, bWA0 = next_slab("wa.0")
        WA1, bWA1 = next_slab("wa.1", hold=1)
        for st in range(NT):
            sl = slice(st * 512, (st + 1) * 512)
            def a_block(blk):
                W, bW, ob = (WA0, bWA0, blk * 1024) if blk < 4 else (WA1, bWA1, (blk - 4) * 1024)
                for c in range(DC):
                    mm(BK[blk][:], W[:, ob + c * 128:ob + (c + 1) * 128], XN[:, c, sl], c == 0, c == DC - 1,
                       [bW, bXN[c][st]], bBK[blk], c == DC - 1)
            for blk in range(4):
                a_block(blk)
            for kk in range(2):
                ob = 2048 + kk * 1024
                for c in range(DC):
                    mm(BK[4 + kk][:], WA1[:, ob + c * 128:ob + (c + 1) * 128], XN[:, c, sl], c == 0, c == DC - 1,
                       [bWA1, bXN[c][st]], bBK[4 + kk], c == DC - 1)
            rope_combine(4, 5, st, KRS[:, sl], bKRS[st])
            sumsq_sbuf(lambda c: BK[c][:], [bBK[c] for c in range(4)], 4, 6, st)
            rstd_to(6, QL, RS[0][:], bRS[0])
            for c in range(4):
                P.op("dve", (lambda c=c, sl=sl: lambda e: e.scalar_tensor_tensor(
                    out=CQ[:, c, sl], in0=BK[c][:], scalar=col("qn", c), in1=RS[0][:], op0=ALU.mult, op1=ALU.mult))(),
                    reads=[bBK[c], bRS[0], bCOLS], writes=[bCQ[c][st]])
            for blk in range(4, 6):
                a_block(blk)
            sumsq_sbuf(lambda c: BK[4 + c][:], [bBK[4 + c] for c in range(2)], 2, 7, st)
            rstd_to(7, KVL, RS[1][:], bRS[1], li=1)
            for c in range(2):
                P.op("dve", (lambda c=c, sl=sl: lambda e: e.scalar_tensor_tensor(
                    out=CKV[:, c, sl], in0=BK[4 + c][:], scalar=col("kvn", c), in1=RS[1][:], op0=ALU.mult,
                    op1=ALU.mult))(),
                    reads=[bBK[4 + c], bRS[1], bCOLS], writes=[bCKV[c][st]])
        P.op("pool", lambda e: e.dma_start(out=krc_d[:, t0:t0 + T], in_=KRS), reads=bKRS, writes=[bRD[i]], dsem="krw")

        WKV, bWKV = next_slab("wukv")
        nb = 0
        for h in range(NH):
            ks = kvctr["kst"] % 4
            kvctr["kst"] += 1
            for st in range(NT):
                sl = slice(st * 512, (st + 1) * 512)
                bk = nb % 4
                nb += 1
                for c in range(2):
                    mm(BK[bk][:], WKV[:, (h * 2 + c) * 128:(h * 2 + c + 1) * 128], CKV[:, c, sl], c == 0, c == 1,
                       [bWKV, bCKV[c][st]], bBK[bk], c == 1)
                if nb % 2 == 0:
                    P.op("act", (lambda ks=ks, sl=sl, bk=bk: lambda e: e.activation(out=KST[ks][:, sl], in_=BK[bk][:],
                                                                                 func=AF.Copy))(),
                         reads=[bBK[bk]], writes=[bKST[ks]], partial=(st > 0))
                else:
                    P.op("dve", (lambda ks=ks, sl=sl, bk=bk: lambda e: e.tensor_copy(out=KST[ks][:, sl], in_=BK[bk][:]))(),
                         reads=[bBK[bk]], writes=[bKST[ks]], partial=(st > 0))
            P.op("pool", (lambda h=h, ks=ks: lambda e: e.dma_start(out=kc_d[:, h, t0:t0 + T], in_=KST[ks]))(),
                 reads=[bKST[ks]], writes=[bKD[i]], dsem=f"kw{ks}", partial=True)
        for tb in range(TB):
            vs = kvctr["vst"] % 4
            kvctr["vst"] += 1
            st = tb // 4
            for half in range(2):
                bk = nb % 4
                nb += 1
                for c in range(2):
                    mm(BK[bk][:], CKV[:, c, tb * 128:(tb + 1) * 128],
                       WKV[:, 2048 + c * 1024 + half * 512:2048 + c * 1024 + (half + 1) * 512], c == 0, c == 1,
                       [bWKV, bCKV[c][st]], bBK[bk], c == 1)
                if nb % 2 == 0:
                    P.op("act", (lambda vs=vs, half=half, bk=bk: lambda e: e.activation(
                        out=VST[vs][:, half * 512:(half + 1) * 512], in_=BK[bk][:], func=AF.Copy))(),
                        reads=[bBK[bk]], writes=[bVST[vs]], partial=(half > 0))
                else:
                    P.op("dve", (lambda vs=vs, half=half, bk=bk: lambda e: e.tensor_copy(
                        out=VST[vs][:, half * 512:(half + 1) * 512], in_=BK[bk][:]))(),
                        reads=[bBK[bk]], writes=[bVST[vs]], partial=(half > 0))
            P.op("pool", (lambda tb=tb, vs=vs: lambda e: e.dma_start(
                out=vc_d[:, :, i * TB + tb, :], in_=VST[vs].rearrange("p (h d) -> p h d", d=128)))(),
                reads=[bVST[vs]], writes=[bVD[i]], dsem=f"vw{vs}", partial=True)

        stages = [(h, j) for h in range(NH) for j in range(i + 1)]
        loaded = {"n": 0}

        def load_stage(idx):
            h, j = stages[idx]
            sidx = kvctr["stage"] + idx
            s = sidx % 2
            P.op("pool", lambda e: e.dma_start(out=STK[s], in_=kc_d[:, h, j * T:(j + 1) * T]),
                 reads=[bKD[j]], writes=[bSTK[s]], dsem=f"sk{s}")
            P.op("pool", lambda e: e.dma_start(out=STR[s], in_=krc_d[:, j * T:(j + 1) * T]),
                 reads=[bRD[j]], writes=[bSTR[s]], dsem=f"sr{s}")
            P.op("pool", lambda e: e.dma_start(out=STV[s].rearrange("p (b d) -> p b d", d=128),
                                               in_=vc_d[:, h, j * TB:(j + 1) * TB, :]),
                 reads=[bVD[j]], writes=[bSTV[s]], dsem=f"sv{s}")

        def ensure_stage(idx):
            while loaded["n"] <= min(idx, len(stages) - 1):
                load_stage(loaded["n"])
                loaded["n"] += 1
        cur = {"idx": -1, "cnt": 0}

        OB = [3, 4]
        DB = [5, 6]
        sidx0 = kvctr["stage"]
        wq = {}

        def q_part(hn, st, part):
            if hn % 2 == 0 and hn not in wq:
                wq[hn] = next_slab(f"wuq.{hn // 2}", hold=(1 if hn > 0 else 0))
                wq[hn + 1] = wq[hn]
            WQ, bWQ = wq[hn]
            hs_ = hn % 2
            sl = slice(st * 512, (st + 1) * 512)
            for c in range(4):
                ob = hs_ * 1536 + part * 512 + c * 128
                mm(BK[7][:], WQ[:, ob:ob + 128], CQ[:, c, sl], c == 0, c == 3, [bWQ, bCQ[c][st]], bBK[7], c == 3)
            if part == 0:
                P.op("act", lambda e: e.activation(out=QN[hs_][:, sl], in_=BK[7][:], func=AF.Copy),
                     reads=[bBK[7]], writes=[bQN[hs_][st]])
            elif part == 1:
                P.op("dve", lambda e: e.tensor_tensor(out=TRG[0][:], in0=BK[7][:], in1=COS[:, sl], op=ALU.mult),
                     reads=[bBK[7], bCOS[st]], writes=[bTRG[0]])
            else:
                P.op("dve", lambda e: e.tensor_tensor(out=TRG[1][:], in0=BK[7][:], in1=SIN[:, sl], op=ALU.mult),
                     reads=[bBK[7], bSIN[st]], writes=[bTRG[1]])
                P.op("dve", lambda e: e.tensor_tensor(out=QR[hs_][:, sl], in0=TRG[0][:], in1=TRG[1][:], op=ALU.add),
                     reads=[bTRG[0], bTRG[1]], writes=[bQR[hs_][st]])

        carry = []

        def flush_carry():
            while carry:
                st, sl, h_ = carry.pop(0)
                mm(BK[7][:], ONES[:], SQ[st % 2][:], True, True, [bONES, bSQ[st % 2]], bBK[7], True)
                P.op("act", (lambda st=st: lambda e: e.activation(out=LNT[st % 2][:], in_=BK[7][:], func=AF.Ln))(),
                     reads=[bBK[7]], writes=[bLNT[st % 2]])
                P.op("act", (lambda st=st: lambda e: e.activation(out=RS[st % 2][:], in_=LNT[st % 2][:], func=AF.Exp,
                                                                  scale=-1.0))(),
                     reads=[bLNT[st % 2]], writes=[bRS[st % 2]])
                P.op("dve", (lambda st=st, sl=sl, h_=h_: lambda e: e.tensor_tensor(
                    out=XN[:, h_, sl], in0=SG[st % 2][:], in1=RS[st % 2][:], op=ALU.mult))(),
                    reads=[bSG[st % 2], bRS[st % 2]], writes=[bXN[h_][st]])

        qparts = [(st, part) for st in range(NT) for part in range(3)]
        for st, part in qparts:
            q_part(0, st, part)
        for h in range(NH):
            hs = h % 2
            items = []
            for j in range(i + 1):
                for kb in range(TB):
                    for st in range(NT):
                        if j == i:
                            if kb > 4 * st + 3:
                                continue
                            off = max(0, (kb - 4 * st) * 128)
                            diag = kb >= 4 * st
                        else:
                            off, diag = 0, False
                        items.append((j, kb, st, off, diag))
            first = {st: True for st in range(NT)}
            lastidx = {}
            for n_, it in enumerate(items):
                lastidx[it[2]] = n_
            pend = []
            deferred = []

            def emit_pv(ent):
                n_, (j, kb, st, off, diag), s, pk = ent
                fl = first[st]
                first[st] = False
                la = lastidx[st] == n_
                mm(BK[OB[st]][:, off:512], STV[s][:, kb * 128:(kb + 1) * 128], PT[pk][:, off:512], fl, la,
                   [bSTV[s], bPT[pk]], bBK[OB[st]], True)
                if fl:
                    assert off == 0
                    P.op("dve", (lambda st=st, pk=pk: lambda e: e.tensor_copy(out=BK[DB[st]][:], in_=PT[pk][:]))(),
                         reads=[bPT[pk]], writes=[bBK[DB[st]]])
                else:
                    P.op("dve", (lambda st=st, pk=pk, off=off: lambda e: e.tensor_tensor(
                        out=BK[DB[st]][:, off:512], in0=BK[DB[st]][:, off:512], in1=PT[pk][:, off:512], op=ALU.add))(),
                        reads=[bPT[pk]], writes=[bBK[DB[st]]])
                if la:
                    sl = slice(st * 512, (st + 1) * 512)
                    P.op("act", (lambda st=st: lambda e: e.activation(out=SG[st % 2][:], in_=BK[OB[st]][:],
                                                                      func=AF.Copy))(),
                         reads=[bBK[OB[st]]], writes=[bSG[st % 2]])
                    P.op("dve", (lambda st=st: lambda e: e.tensor_copy(out=SQ[st % 2][:], in_=BK[DB[st]][:]))(),
                         reads=[bBK[DB[st]]], writes=[bSQ[st % 2]])
                    deferred.append((st, sl, h))

            nxt = list(qparts) if h + 1 < NH else []
            spacing = max(1, len(items) // 7)
            for n_, it in enumerate(items):
                j, kb, st, off, diag = it
                if nxt and n_ > 0 and n_ % spacing == 0:
                    q_part(h + 1, *nxt.pop(0))
                if n_ == 3:
                    flush_carry()
                idx = h * (i + 1) + j
                if idx != cur["idx"]:
                    cur["idx"] = idx
                    cur["cnt"] = 0
                    ensure_stage(idx)
                cur["cnt"] += 1
                s = (sidx0 + idx) % 2
                sb_ = kvctr["sbank"] % 3
                kvctr["sbank"] += 1
                pk = kvctr["pt"] % 4
                kvctr["pt"] += 1
                q0 = st * 512 + off
                q1 = (st + 1) * 512
                mm(BK[sb_][:, off:512], STK[s][:, kb * 128:(kb + 1) * 128], QN[hs][:, q0:q1], True, False,
                   [bSTK[s], bQN[hs][st]], bBK[sb_], False)
                mm(BK[sb_][:, off:512], STR[s][:, kb * 128:(kb + 1) * 128], QR[hs][:, q0:q1], False, True,
                   [bSTR[s], bQR[hs][st]], bBK[sb_], True)
                P.op("act", (lambda sb_=sb_, pk=pk, off=off: lambda e: e.activation(
                    out=PT[pk][:, off:512], in_=BK[sb_][:, off:512], func=AF.Exp, scale=SCALE))(),
                    reads=[bBK[sb_]], writes=[bPT[pk]])
                if diag:
                    P.op("dve", (lambda pk=pk, off=off: lambda e: e.memset(PT[pk][64:128, off:off + 64], 0.0))(),
                         writes=[bPT[pk]], partial=True)
                pend.append((n_, it, s, pk))
                if len(pend) > 2:
                    emit_pv(pend.pop(0))
                if cur["cnt"] == 3:
                    ensure_stage(idx + 1)
            while pend:
                emit_pv(pend.pop(0))
            carry.extend(deferred)
            if h == NH - 1:
                flush_carry()
            while nxt:
                q_part(h + 1, *nxt.pop(0))
        kvctr["stage"] += len(stages)
        if DEBUG:
            P.op("pool", lambda e: e.dma_start(out=dbg_d[:, :], in_=XN[:].rearrange("p c t -> p (c t)")),
                 reads=[b for r_ in bXN for b in r_], dsem="dbg")
            P.op("pool", lambda e: e.dma_start(out=dbg2_d[:, :], in_=AR[:, 0:6 * T]),
                 reads=[b for r_ in bCQ + bCKV for b in r_], dsem="dbg2")

        n = 0
        for s2 in range(2):
            W, bW = next_slab(f"wo.{s2}")
            for j in range(4):
                kb = 4 * s2 + j
                for st in range(NT):
                    sl = slice(st * 512, (st + 1) * 512)
                    by = n % 3
                    n += 1
                    for hh in range(NH):
                        mm(BK[by][:], W[:, (j * 8 + hh) * 128:(j * 8 + hh + 1) * 128], XN[:, hh, sl], hh == 0, hh == NH - 1,
                           [bW, bXN[hh][st]], bBK[by], hh == NH - 1)
                    P.op("dve", (lambda kb=kb, sl=sl, by=by: lambda e: e.tensor_tensor(
                        out=H[:, kb, sl], in0=BK[by][:], in1=H[:, kb, sl], op=ALU.add))(),
                        reads=[bBK[by]], writes=[bH[kb][st]])
                    norm_push(kb, st)

    def load_x_block(i, tb, slot):
        t0 = i * T + tb * 128
        P.op("pool", lambda e: e.dma_start(out=IOB[slot][:], in_=x_d[t0:t0 + 128, :]),
             writes=[bIOB[slot]], dsem=f"io{slot}")

    ioctr = {"n": 0}

    for i in range(NTILES):
        norm_begin(xn_target("fn1", 0))
        for tb in range(TB):
            slot = ioctr["n"] % 4
            ioctr["n"] += 1
            load_x_block(i, tb, slot)
            st = tb // 4
            for half in range(2):
                bk = rr["bank"] % 4
                rr["bank"] += 1
                for j in range(4):
                    c = half * 4 + j
                    P.op("pe", (lambda c=c, j=j, bk=bk, slot=slot: lambda e: e.transpose(
                        BK[bk][:, j * 128:(j + 1) * 128], IOB[slot][:, c * 128:(c + 1) * 128], IDENT[:]))(),
                        reads=[bIOB[slot], bIDENT], writes=[bBK[bk]], sig=(j == 3))
                P.op("act", (lambda half=half, tb=tb, bk=bk: lambda e: e.activation(
                    out=H[:, half * 4:half * 4 + 4, tb * 128:(tb + 1) * 128],
                    in_=BK[bk][:].rearrange("p (a b) -> p a b", b=128), func=AF.Copy))(),
                    reads=[bBK[bk]], writes=[bH[half * 4 + j][st] for j in range(4)], partial=True)
            if tb % 4 == 3:
                for c in range(DC):
                    norm_push(c, st)
        to_XN("fn1", 0)
        norm_begin(xn_target("mix", 0))
        ffn("L0f1", extra=trig_tables(i))
        to_XN("mix", 0)
        norm_begin(None)
        UW = HALO + T
        U = AR[:, 0:DC * UW].rearrange("p (c t) -> p c t", t=UW)
        cbase = DC * UW
        Cf = AR[:, cbase:cbase + 2 * DC * 512].bitcast(F32).rearrange("p (c t) -> p c t", t=512)
        leg = legacy_of(arena["bufs"])
        bUh = [Buf(f"Uh{c}", leg) for c in range(DC)]
        bU = [[Buf(f"U{c}.{s}", leg) for s in range(NT)] for c in range(DC)]
        bC = [Buf(f"C{c}", leg) for c in range(DC)]
        arena["bufs"] = bUh + [b for r_ in bU for b in r_] + bC
        for c in range(DC):
            P.op("dve", (lambda c=c: lambda e: e.tensor_copy(out=U[:, c, 0:HALO], in_=UH[:, c, :]))(),
                 reads=[bUH], writes=[bUh[c]])
        n = 0
        for s in range(4):
            W, bW = next_slab(f"pw1.{s}")
            for j in range(2):
                ci = 2 * s + j
                for st in range(NT):
                    sl = slice(st * 512, (st + 1) * 512)
                    ba, bb = (n % 2), 2 + (n % 2)
                    n += 1
                    for c in range(DC):
                        o_ = (j * 16 + c) * 128
                        mm(BK[ba][:], W[:, o_:o_ + 128], XN[:, c, sl], c == 0, c == DC - 1, [bW, bXN[c][st]], bBK[ba],
                           c == DC - 1)
                    for c in range(DC):
                        o_ = (j * 16 + 8 + c) * 128
                        mm(BK[bb][:], W[:, o_:o_ + 128], XN[:, c, sl], c == 0, c == DC - 1, [bW, bXN[c][st]], bBK[bb],
                           c == DC - 1)
                    k = rr["sg"] % 2
                    rr["sg"] += 1
                    P.op("act", (lambda bb=bb, k=k: lambda e: e.activation(out=SG[k][:], in_=BK[bb][:], func=AF.Sigmoid))(),
                         reads=[bBK[bb]], writes=[bSG[k]])
                    P.op("dve", (lambda ci=ci, st=st, ba=ba, k=k: lambda e: e.tensor_tensor(
                        out=U[:, ci, HALO + st * 512:HALO + (st + 1) * 512], in0=BK[ba][:], in1=SG[k][:], op=ALU.mult))(),
                        reads=[bBK[ba], bSG[k]], writes=[bU[ci][st]])
        CfS = [Cf]
        bCS = [bC]
        if NT > 1:
            c2 = cbase + 2 * DC * 512
            assert c2 + 2 * DC * 512 <= ARW, "arena too small for conv"
            CfS.append(AR[:, c2:c2 + 2 * DC * 512].bitcast(F32).rearrange("p (c t) -> p c t", t=512))
            bC2 = [Buf(f"C2{c}", leg) for c in range(DC)]
            bCS.append(bC2)
            arena["bufs"] += bC2
        for ci in range(DC):
            W, bW = next_slab(f"dw.{ci}")
            for st in range(NT):
                bc = 4 + (n % 2)
                n += 1
                rd = [bW, bU[ci][st]] + ([bUh[ci]] if st == 0 else [bU[ci][st - 1]])
                for j in range(CW):
                    c0 = st * 512 + 2 + j
                    mm(BK[bc][:], W[:, j * 128:(j + 1) * 128], U[:, ci, c0:c0 + 512], j == 0, j == CW - 1, rd, bBK[bc],
                       j == CW - 1)
                P.op("act", (lambda ci=ci, st=st, bc=bc: lambda e: e.activation(
                    out=CfS[st][:, ci, :], in_=BK[bc][:], func=AF.Copy))(),
                    reads=[bBK[bc]], writes=[bCS[st][ci]])
                norm_push_src(CfS[st][:, ci, :], bCS[st][ci], st)
        for c in range(DC):
            P.op("dve", (lambda c=c: lambda e: e.tensor_copy(out=UH[:, c, :], in_=U[:, c, T:T + HALO]))(),
                 reads=[bU[c][NT - 1]], writes=[bUH], partial=True)
        norm_stats_finish()
        norm_begin(xn_target("fn2", 0))
        for st in range(NT):
            sl = slice(st * 512, (st + 1) * 512)
            for c in range(DC):
                P.op("dve", (lambda c=c, st=st: lambda e: e.scalar_tensor_tensor(
                    out=CfS[st][:, c, :], in0=CfS[st][:, c, :], scalar=col("cnorm", c), in1=BK[7 - st][:],
                    op0=ALU.mult, op1=ALU.mult))(),
                    reads=[bBK[7 - st], bCOLS], writes=[bCS[st][c]])
                P.op("act", (lambda c=c, st=st, sl=sl: lambda e: e.activation(
                    out=XN[:, c, sl], in_=CfS[st][:, c, :], func=AF.Silu))(),
                    reads=[bCS[st][c]], writes=[bXN[c][st]])
        for s in range(2):
            W, bW = next_slab(f"pw2.{s}")
            for j in range(4):
                kb = 4 * s + j
                for st in range(NT):
                    sl = slice(st * 512, (st + 1) * 512)
                    by = 4 + (n % 2)
                    n += 1
                    for c in range(DC):
                        mm(BK[by][:], W[:, (j * 8 + c) * 128:(j * 8 + c + 1) * 128], XN[:, c, sl], c == 0, c == DC - 1,
                           [bW, bXN[c][st]], bBK[by], c == DC - 1)
                    P.op("dve", (lambda kb=kb, sl=sl, by=by: lambda e: e.tensor_tensor(
                        out=H[:, kb, sl], in0=BK[by][:], in1=H[:, kb, sl], op=ALU.add))(),
                        reads=[bBK[by]], writes=[bH[kb][st]])
                    norm_push(kb, st)
        to_XN("fn2", 0)
        norm_begin(xn_target("fn1", 8))
        ffn("L0f2")
        to_XN("fn1", 8)
        norm_begin(xn_target("mix", 8))
        ffn("L1f1")
        to_XN("mix", 8)
        norm_begin(xn_target("fn2", 8))
        mla(i)
        to_XN("fn2", 8)
        norm_begin(("final", 0, lambda c, sl: H[:, c, sl], lambda c, st: bH[c][st], True))
        ffn("L1f2")
        norm_stats_finish()
        for tb in range(TB):
            st = tb // 4
            slot = ioctr["n"] % 4
            ioctr["n"] += 1
            for half in range(2):
                bk = rr["bank"] % 4
                rr["bank"] += 1
                for j in range(4):
                    c = half * 4 + j
                    P.op("pe", (lambda c=c, j=j, bk=bk, tb=tb: lambda e: e.transpose(
                        BK[bk][:, j * 128:(j + 1) * 128], H[:, c, tb * 128:(tb + 1) * 128], IDENT[:]))(),
                        reads=[bH[c][st], bIDENT], writes=[bBK[bk]], sig=(j == 3))
                P.op("act", (lambda half=half, bk=bk, slot=slot: lambda e: e.activation(
                    out=IOB[slot][:, half * 512:(half + 1) * 512], in_=BK[bk][:], func=AF.Copy))(),
                    reads=[bBK[bk]], writes=[bIOB[slot]], partial=(half == 1))
            t0 = i * T + tb * 128
            P.op("pool", (lambda t0=t0, slot=slot: lambda e: e.dma_start(out=out_d[t0:t0 + 128, :], in_=IOB[slot][:]))(),
                 reads=[bIOB[slot]], dsem=f"io{slot}")

    sems = {}
    for k in sorted(P.semkeys):
        sems[k] = es.enter_context(nc.semaphore(k))
    final_waits = [(k, P.semcnt[k]) for k in ("io0", "io1", "io2", "io3")]
    with nc.Block() as block:
        def replay(E, eng, extra=()):
            for waits, fn, inc in E.ops:
                for k, v in waits:
                    eng.wait_ge(sems[k], v)
                ins = fn(eng)
                if inc is not None:
                    ins.then_inc(sems[inc[0]], inc[1])
            for k, v in extra:
                eng.wait_ge(sems[k], v)

        @block.tensor
        def _(e):
            replay(P.q["pe"], e)

        @block.scalar
        def _(e):
            replay(P.q["act"], e)

        @block.vector
        def _(e):
            replay(P.q["dve"], e)

        @block.sync
        def _(e):
            replay(P.q["sp"], e)

        @block.gpsimd
        def _(e):
            replay(P.q["pool"], e, extra=final_waits)
    es.close()
    return nc


_CACHE = {}


def _get_program(SEQ):
    if SEQ not in _CACHE:
        _CACHE[SEQ] = build_program(SEQ)
    return _CACHE[SEQ]


def kernel(**inputs):
    p = {k: np.asarray(v) for k, v in inputs.items()}
    x = p["x"]
    B, SEQ, _ = x.shape
    assert B == 8 and SEQ % T == 0
    nc = _get_program(SEQ)
    wall = pack_weights(p)
    cols = pack_cols(p)
    ident = np.eye(128, dtype=np.float32)
    pos = p["positions"].astype(np.int32)
    in_maps = [{"x": np.ascontiguousarray(x[b]), "pos": np.ascontiguousarray(pos[b:b + 1]), "wall": wall,
                "cols": cols, "ident": ident} for b in range(B)]
    res = run_bass_kernel_spmd(nc, in_maps, core_ids=list(range(B)))
    return np.stack([np.asarray(r["out"]) for r in res.results], axis=0).astype(np.float32)
```

```python
import numpy as np
from contextlib import ExitStack
import concourse.bass as bass
import concourse.mybir as mybir
from concourse.bass_utils import run_bass_kernel_spmd

F32 = mybir.dt.float32
BF16 = mybir.dt.bfloat16
I32 = mybir.dt.int32
AF = mybir.ActivationFunctionType
ALU = mybir.AluOpType

D = 1024
DC = 8
DFF = 2816
NF = 22
CW = 31
NH = 8
QL, KVL, ROPE = 512, 256, 64
EPS = 1e-6
SCALE = float((128 + 64) ** -0.5)
NT = 2
T = 512 * NT
TB = T // 128
SLABW = 4096
NSLOT = 4
HALO = 32
DEBUG = False


def _stream_layout():
    sl = []
    for L in range(2):
        for nm in ("f1", "f2"):
            pass
    def ffn(tag):
        return [(f"{tag}.w13.{i}", 4096) for i in range(11)] + [(f"{tag}.w2.{k}", NF * 128) for k in range(8)]
    sl += ffn("L0f1")
    sl += [(f"pw1.{s}", 4096) for s in range(4)]
    sl += [(f"dw.{i}", CW * 128) for i in range(8)]
    sl += [(f"pw2.{s}", 4096) for s in range(2)]
    sl += ffn("L0f2")
    sl += ffn("L1f1")
    sl += [("wa.0", 4096), ("wa.1", 4096), ("wukv", 4096)]
    sl += [(f"wuq.{s}", 3072) for s in range(4)]
    sl += [(f"wo.{s}", 4096) for s in range(2)]
    sl += ffn("L1f2")
    offs = {}
    o = 0
    for n, w in sl:
        offs[n] = (o, w)
        o += w
    return sl, offs, o


STREAM, SOFF, TOTW = _stream_layout()

COL = {}
_c = 0
for _n, _w in [("fn1", 16), ("mix", 16), ("fn2", 16), ("cnorm", 8), ("final", 8), ("qn", 4), ("kvn", 2),
               ("invf", 1), ("sgn", 1)]:
    COL[_n] = _c
    _c += _w
NCOL = _c


def _blk(W, m, mw=128):
    K = W.shape[0]
    kc = K // 128
    return np.ascontiguousarray(W[:, m * mw:(m + 1) * mw].reshape(kc, 128, mw).transpose(1, 0, 2)).reshape(128, kc * mw)


def _pad_cols(Wk64):
    out = np.zeros((Wk64.shape[0], 128), np.float32)
    out[:, :64] = Wk64
    return out


def pack_weights(p):
    wall = np.zeros((128, TOTW), np.float32)

    def put(name, arr):
        o, w = SOFF[name]
        assert arr.shape == (128, w), (name, arr.shape, w)
        wall[:, o:o + w] = arr

    def ffn(tag, w1, w3, w2):
        for i in range(11):
            put(f"{tag}.w13.{i}", np.concatenate(
                [_blk(w1, 2 * i), _blk(w1, 2 * i + 1), _blk(w3, 2 * i), _blk(w3, 2 * i + 1)], axis=1))
        for k in range(8):
            put(f"{tag}.w2.{k}", _blk(w2, k))

    ffn("L0f1", p["ffn1_w1"][0], p["ffn1_w3"][0], p["ffn1_w2"][0])
    ffn("L0f2", p["ffn2_w1"][0], p["ffn2_w3"][0], p["ffn2_w2"][0])
    ffn("L1f1", p["ffn1_w1"][1], p["ffn1_w3"][1], p["ffn1_w2"][1])
    ffn("L1f2", p["ffn2_w1"][1], p["ffn2_w3"][1], p["ffn2_w2"][1])
    pw1 = p["conv_w_pw1"][0]
    for s in range(4):
        put(f"pw1.{s}", np.concatenate(
            [_blk(pw1, 2 * s), _blk(pw1, 8 + 2 * s), _blk(pw1, 2 * s + 1), _blk(pw1, 8 + 2 * s + 1)], axis=1))
    wdw = p["conv_w_dw"][0]
    for i in range(8):
        dg = np.zeros((128, CW, 128), np.float32)
        idx = np.arange(128)
        dg[idx, :, idx] = wdw[:, i * 128:(i + 1) * 128].T
        put(f"dw.{i}", dg.reshape(128, CW * 128))
    pw2 = p["conv_w_pw2"][0]
    for s in range(2):
        put(f"pw2.{s}", np.concatenate([_blk(pw2, 4 * s + j) for j in range(4)], axis=1))
    wa = p["mla_w_a"][0]
    put("wa.0", np.concatenate([_blk(wa, j) for j in range(4)], axis=1))
    kr = wa[:, 768:832]
    krp = np.concatenate([kr[:, 32:], kr[:, :32]], axis=1)
    put("wa.1", np.concatenate([_blk(wa, 4), _blk(wa, 5), _blk(_pad_cols(kr), 0), _blk(_pad_cols(krp), 0)], axis=1))
    ukv = p["mla_w_ukv"][0]
    uk = np.concatenate([_blk(np.ascontiguousarray(ukv[:, h, :128]), 0) for h in range(NH)], axis=1)
    uv = np.ascontiguousarray(ukv[:, :, 128:]).reshape(256, NH * 128)
    uvl = np.ascontiguousarray(uv.reshape(2, 128, 1024).transpose(1, 0, 2)).reshape(128, 2048)
    put("wukv", np.concatenate([uk, uvl], axis=1))
    uq = p["mla_w_uq"][0]
    for s in range(4):
        parts = []
        for h in (2 * s, 2 * s + 1):
            qn = np.ascontiguousarray(uq[:, h, :128])
            qr = uq[:, h, 128:]
            qrp = np.concatenate([qr[:, 32:], qr[:, :32]], axis=1)
            parts += [_blk(qn, 0), _blk(_pad_cols(qr), 0), _blk(_pad_cols(qrp), 0)]
        put(f"wuq.{s}", np.concatenate(parts, axis=1))
    wo = p["mla_w_o"][0]
    for s in range(2):
        put(f"wo.{s}", np.concatenate([_blk(wo, 4 * s + j) for j in range(4)], axis=1))
    return wall


def pack_cols(p):
    cols = np.zeros((128, NCOL), np.float32)

    def colv(v):
        return np.ascontiguousarray(np.asarray(v, np.float32).reshape(-1, 128).T)

    for L in range(2):
        cols[:, COL["fn1"] + 8 * L:COL["fn1"] + 8 * L + 8] = colv(p["ffn_norm1"][L])
        cols[:, COL["mix"] + 8 * L:COL["mix"] + 8 * L + 8] = colv(p["mix_norm"][L])
        cols[:, COL["fn2"] + 8 * L:COL["fn2"] + 8 * L + 8] = colv(p["ffn_norm2"][L])
    cols[:, COL["cnorm"]:COL["cnorm"] + 8] = colv(p["conv_norm"][0])
    cols[:, COL["final"]:COL["final"] + 8] = colv(p["final_norm"])
    cols[:, COL["qn"]:COL["qn"] + 4] = colv(p["mla_q_norm"][0])
    cols[:, COL["kvn"]:COL["kvn"] + 2] = colv(p["mla_kv_norm"][0])
    import jax
    import jax.numpy as jnp
    with jax.default_device(jax.devices("cpu")[0]):
        invf = np.asarray(10000.0 ** (-2.0 * jnp.arange(32, dtype=jnp.float32) / 64))
    cols[:, COL["invf"]] = np.tile(invf, 4)
    sg = np.ones(128, np.float32)
    sg[0:32] = -1.0
    sg[64:96] = -1.0
    cols[:, COL["sgn"]] = sg
    return cols


class Buf:
    __slots__ = ("name", "w", "r")

    def __init__(self, name, legacy=None):
        self.name = name
        self.w = {}
        self.r = dict(legacy) if legacy else {}


class Q:
    def __init__(self, name, kind):
        self.name = name
        self.kind = kind
        self.ops = []
        self.count = 0
        self.waited = {}
        self.semkey = "e_" + name


class Prog:
    def __init__(self):
        self.q = {"pe": Q("pe", "pe"), "act": Q("act", "c"), "dve": Q("dve", "c"),
                  "sp": Q("sp", "dma"), "pool": Q("pool", "dma")}
        self.semcnt = {}
        self.semkeys = set(q.semkey for q in self.q.values() if q.kind != "dma")

    def op(self, qn, fn, reads=(), writes=(), sig=True, dsem=None, partial=False):
        E = self.q[qn]
        deps = {}

        def add(d):
            for k, v in d.items():
                if deps.get(k, 0) < v:
                    deps[k] = v
        for b in reads:
            add(b.w)
        for b in writes:
            add(b.w)
            add(b.r)
        waits = []
        for k, v in deps.items():
            if E.kind == "pe" and k == E.semkey:
                continue
            if E.waited.get(k, 0) < v:
                waits.append((k, v))
                E.waited[k] = v
        if E.kind == "dma":
            assert dsem is not None
            c = self.semcnt.get(dsem, 0) + 16
            self.semcnt[dsem] = c
            self.semkeys.add(dsem)
            ev = (dsem, c)
            inc = (dsem, 16)
        elif sig:
            E.count += 1
            ev = (E.semkey, E.count)
            inc = (E.semkey, 1)
        else:
            ev = (E.semkey, E.count + 1)
            inc = None
        assert ev[1] < 60000, "semaphore value too large"
        E.ops.append((waits, fn, inc))
        for b in reads:
            if b.r.get(ev[0], 0) < ev[1]:
                b.r[ev[0]] = ev[1]
        for b in writes:
            if partial:
                if b.w.get(ev[0], 0) < ev[1]:
                    b.w[ev[0]] = ev[1]
            else:
                b.w = {ev[0]: ev[1]}
            b.r = {}
        return ev


def legacy_of(bufs):
    leg = {}
    for b in bufs:
        for d in (b.w, b.r):
            for k, v in d.items():
                if leg.get(k, 0) < v:
                    leg[k] = v
    return leg


def build_program(SEQ):
    NTILES = SEQ // T
    NBLK = SEQ // 128
    nc = bass.Bass("TRN2", target_bir_lowering=False)
    x_d = nc.dram_tensor("x", [SEQ, D], F32, kind="ExternalInput").ap()
    pos_d = nc.dram_tensor("pos", [1, SEQ], I32, kind="ExternalInput").ap()
    wall_d = nc.dram_tensor("wall", [128, TOTW], F32, kind="ExternalInput").ap()
    cols_d = nc.dram_tensor("cols", [128, NCOL], F32, kind="ExternalInput").ap()
    ident_d = nc.dram_tensor("ident", [128, 128], F32, kind="ExternalInput").ap()
    out_d = nc.dram_tensor("out", [SEQ, D], F32, kind="ExternalOutput").ap()
    wbf_d = nc.dram_tensor("wbf", [128, TOTW], BF16, kind="Internal").ap()
    kc_d = nc.dram_tensor("kc", [128, NH, SEQ], BF16, kind="Internal").ap()
    krc_d = nc.dram_tensor("krc", [128, SEQ], BF16, kind="Internal").ap()
    vc_d = nc.dram_tensor("vc", [128, NH, NBLK, 128], BF16, kind="Internal").ap()
    dbg_d = nc.dram_tensor("dbg", [128, DC * T], BF16, kind="Internal").ap() if DEBUG else None
    dbg2_d = nc.dram_tensor("dbg2", [128, 6 * T], BF16, kind="Internal").ap() if DEBUG else None

    P = Prog()
    es = ExitStack()

    def sb(name, shape, dt):
        return es.enter_context(nc.sbuf_tensor(name, shape, dt))

    H = sb("H", [128, DC, T], F32)
    XN = sb("XN", [128, DC, T], BF16)
    ARW = 28672
    AR = sb("AR", [128, ARW], BF16)
    WR = [sb(f"WR{i}", [128, SLABW], BF16) for i in range(NSLOT)]
    IOB = [sb(f"IOB{i}", [128, D], F32) for i in range(4)]
    COLS = sb("COLS", [128, NCOL], F32)
    IDENT = sb("IDENT", [128, 128], F32)
    ONES = sb("ONES", [128, 128], BF16)
    SQ = [sb(f"SQ{i}", [128, 512], BF16) for i in range(4)]
    SG = [sb(f"SG{i}", [128, 512], F32) for i in range(2)]
    LNT = [sb(f"LNT{i}", [128, 512], F32) for i in range(2)]
    RS = [sb(f"RS{i}", [128, 512], F32) for i in range(2)]
    UH = sb("UH", [128, DC, HALO], BF16)
    TRG = [sb(f"TRG{i}", [128, 512], F32) for i in range(3)]
    POSI = sb("POSI", [128, 512], I32)
    COS = sb("COS", [128, T], F32)
    SIN = sb("SIN", [128, T], F32)
    BK = [es.enter_context(nc.psum_tensor(f"BK{i}", [128, 512], F32)) for i in range(8)]

    bH = [[Buf(f"H{c}.{s}") for s in range(NT)] for c in range(DC)]
    bXN = [[Buf(f"XN{c}.{s}") for s in range(NT)] for c in range(DC)]
    bWR = [Buf(f"WR{i}") for i in range(NSLOT)]
    bIOB = [Buf(f"IOB{i}") for i in range(4)]
    bBK = [Buf(f"BK{i}") for i in range(8)]
    bCOLS, bIDENT, bONES = Buf("cols"), Buf("ident"), Buf("ones")
    bSQ = [Buf(f"sq{i}") for i in range(4)]
    bSG = [Buf("sg0"), Buf("sg1")]
    bLNT = [Buf("lnt0"), Buf("lnt1")]
    bRS = [Buf("rs0"), Buf("rs1")]
    bUH = Buf("uh")
    bTRG = [Buf(f"trg{i}") for i in range(3)]
    bPOSI = Buf("posi")
    bCOS = [Buf(f"cos{s}") for s in range(NT)]
    bSIN = [Buf(f"sin{s}") for s in range(NT)]
    bWBF = Buf("wbf")
    bKD = [Buf(f"kd{j}") for j in range(NTILES)]
    bRD = [Buf(f"rd{j}") for j in range(NTILES)]
    bVD = [Buf(f"vd{j}") for j in range(NTILES)]

    def col(name, j=0):
        c = COL[name] + j
        return COLS[:, c:c + 1]

    P.op("sp", lambda e: e.dma_start(out=COLS[:], in_=cols_d[:]), writes=[bCOLS], dsem="setup0")
    P.op("sp", lambda e: e.dma_start(out=IDENT[:], in_=ident_d[:]), writes=[bIDENT], dsem="setup1")
    P.op("dve", lambda e: e.memset(ONES[:], 1.0), writes=[bONES])
    P.op("dve", lambda e: e.memset(UH[:], 0.0), writes=[bUH])
    NPRE = 32
    step = (TOTW + NPRE - 1) // NPRE
    step = (step + 1023) // 1024 * 1024
    pre_chunks = []
    o = 0
    while o < TOTW:
        w = min(step, TOTW - o)
        pre_chunks.append((o, o + w, Buf(f"wbf{len(pre_chunks)}")))
        o += w
    pre_state = {"issued": 0}

    def issue_prepass_upto(k):
        while pre_state["issued"] < min(k, len(pre_chunks)):
            n_ = pre_state["issued"]
            c0, c1, b = pre_chunks[n_]
            P.op("pool", (lambda c0=c0, c1=c1: lambda e: e.dma_start(out=wbf_d[:, c0:c1], in_=wall_d[:, c0:c1],
                                                                      max_dma_last_dim=4096))(),
                 writes=[b], dsem=f"pre{n_}")
            pre_state["issued"] += 1

    wstate = {"n": 0, "loaded": 0}
    total_slabs = NTILES * len(STREAM)

    def issue_load(m):
        name, w = STREAM[m % len(STREAM)]
        off = SOFF[name][0]
        slot = m % NSLOT
        need = [n_ for n_, (c0, c1, b) in enumerate(pre_chunks) if c0 < off + w and off < c1]
        issue_prepass_upto(max(need) + 3)
        rd = [pre_chunks[n_][2] for n_ in need]
        P.op("sp", lambda e: e.dma_start(out=WR[slot][:, 0:w], in_=wbf_d[:, off:off + w]),
             reads=rd, writes=[bWR[slot]], dsem=f"wr{slot}")

    def next_slab(expect, hold=0):
        n = wstate["n"]
        name, w = STREAM[n % len(STREAM)]
        assert name.endswith(expect) or expect in name, (name, expect)
        while wstate["loaded"] < min(total_slabs, max(n + 1, n + NSLOT - hold)):
            issue_load(wstate["loaded"])
            wstate["loaded"] += 1
        wstate["n"] = n + 1
        return WR[n % NSLOT], bWR[n % NSLOT]

    rr = {"sq": 0, "sg": 0, "bank": 0}
    arena = {"bufs": []}

    mmhook = {"n": None, "fn": None}

    def pe_tick():
        if mmhook["n"] is not None:
            mmhook["n"] -= 1
            if mmhook["n"] <= 0:
                fn = mmhook["fn"]
                mmhook["n"] = None
                mmhook["fn"] = None
                fn()

    def mm(out, lhsT, rhs, start, stop, reads, wbuf, sig):
        P.op("pe", lambda e: e.matmul(out, lhsT, rhs, start=start, stop=stop), reads=reads, writes=[wbuf], sig=sig)
        pe_tick()

    def rstd_to(bank, Dn, dst_ap, dst_buf, li=0):
        P.op("act", lambda e: e.activation(out=LNT[li][:], in_=BK[bank][:], func=AF.Ln, scale=1.0 / Dn, bias=EPS),
             reads=[bBK[bank]], writes=[bLNT[li]])
        P.op("act", lambda e: e.activation(out=dst_ap, in_=LNT[li][:], func=AF.Exp, scale=-0.5),
             reads=[bLNT[li]], writes=[dst_buf])

    def sumsq_sbuf(src_ap_fn, src_bufs, nchunks, bank, st):
        for c in range(nchunks):
            k = rr["sq"] % 4
            rr["sq"] += 1
            src = src_ap_fn(c)
            P.op("act", (lambda src=src, k=k: lambda e: e.activation(out=SQ[k][:], in_=src, func=AF.Square))(),
                 reads=[src_bufs[c]], writes=[bSQ[k]])
            mm(BK[bank][:], ONES[:], SQ[k][:], c == 0, c == nchunks - 1, [bONES, bSQ[k]], bBK[bank], True)

    nstate = {"pend": [], "cnt": [0] * NT, "target": None, "done": [False] * NT}

    def xn_target(gname, gofs):
        return (gname, gofs, lambda c, sl: XN[:, c, sl], lambda c, st: bXN[c][st], False)

    def norm_begin(target):
        if mmhook["n"] is not None:
            nstate["next_target"] = (target,)
            return
        assert not nstate["pend"] and nstate["cnt"] == [0] * NT
        nstate["target"] = target

    def _norm_complete(st):
        rstd_to(7 - st, D, BK[7 - st][:], bBK[7 - st], li=st)
        nstate["done"][st] = True
        tgt = nstate["target"]
        if tgt is None:
            return
        gname, gofs, dst_fn, dst_bufs_fn, dst_is_H = tgt
        sl = slice(st * 512, (st + 1) * 512)
        for c in range(DC):
            dst = dst_fn(c, sl)
            P.op("dve", (lambda c=c, sl=sl, dst=dst, st=st, gname=gname, gofs=gofs: lambda e: e.scalar_tensor_tensor(
                out=dst, in0=H[:, c, sl], scalar=col(gname, gofs + c), in1=BK[7 - st][:],
                op0=ALU.mult, op1=ALU.mult))(),
                reads=[bH[c][st], bBK[7 - st], bCOLS] if not dst_is_H else [bBK[7 - st], bCOLS],
                writes=[dst_bufs_fn(c, st)])

    def _norm_flush_one():
        k, st = nstate["pend"].pop(0)
        first = nstate["cnt"][st] == 0
        nstate["cnt"][st] += 1
        last = nstate["cnt"][st] == DC
        mm(BK[7 - st][:], ONES[:], SQ[k][:], first, last, [bONES, bSQ[k]], bBK[7 - st], True)
        if last:
            _norm_complete(st)

    def norm_flush_pending():
        while nstate["pend"]:
            _norm_flush_one()

    def norm_push_src(src, src_buf, st):
        k = rr["sq"] % 4
        rr["sq"] += 1
        P.op("act", (lambda src=src, k=k: lambda e: e.activation(out=SQ[k][:], in_=src, func=AF.Square))(),
             reads=[src_buf], writes=[bSQ[k]])
        nstate["pend"].append((k, st))
        if len(nstate["pend"]) > 2:
            _norm_flush_one()

    def norm_push(c, st):
        norm_push_src(H[:, c, st * 512:(st + 1) * 512], bH[c][st], st)

    def norm_stats_finish():
        norm_flush_pending()
        assert nstate["cnt"] == [DC] * NT and all(nstate["done"]), (nstate["cnt"], nstate["done"])
        nstate["cnt"] = [0] * NT
        nstate["done"] = [False] * NT
        nstate["target"] = None

    def norm_finish_deferred(after=6):
        def _fin():
            norm_stats_finish()
            nt = nstate.pop("next_target", None)
            if nt is not None:
                norm_begin(nt[0])
        while len(nstate["pend"]) > 1:
            _norm_flush_one()
        if (not nstate["pend"] or nstate["pend"][0][1] != NT - 1
                or not all(nstate["done"][s] for s in range(NT - 1))):
            _fin()
            return
        assert mmhook["n"] is None
        mmhook["n"] = after
        mmhook["fn"] = _fin

    def to_XN(gname, gofs):
        assert nstate["target"] is not None and nstate["target"][:2] == (gname, gofs), (nstate["target"], gname, gofs)
        norm_finish_deferred()

    def ffn(tag, extra=None):
        G = AR[:, 0:NF * T].rearrange("p (f t) -> p f t", t=T)
        leg = legacy_of(arena["bufs"])
        bG = [[Buf(f"G{f}.{s}", leg) for s in range(NT)] for f in range(NF)]
        arena["bufs"] = [b for row in bG for b in row]
        n = 0
        for i in range(11):
            W, bW = next_slab(f"{tag}.w13.{i}")
            for st in range(NT):
                for j in range(2):
                    f = 2 * i + j
                    sl = slice(st * 512, (st + 1) * 512)
                    b1, b3 = (n % 2), 2 + (n % 2)
                    n += 1
                    for c in range(DC):
                        mm(BK[b1][:], W[:, (j * 8 + c) * 128:(j * 8 + c + 1) * 128], XN[:, c, sl], c == 0, c == DC - 1,
                           [bW, bXN[c][st]], bBK[b1], c == DC - 1)
                        mm(BK[b3][:], W[:, 2048 + (j * 8 + c) * 128:2048 + (j * 8 + c + 1) * 128], XN[:, c, sl],
                           c == 0, c == DC - 1, [bW, bXN[c][st]], bBK[b3], c == DC - 1)
                    k = rr["sg"] % 2
                    rr["sg"] += 1
                    P.op("act", (lambda b1=b1, k=k: lambda e: e.activation(out=SG[k][:], in_=BK[b1][:], func=AF.Silu))(),
                         reads=[bBK[b1]], writes=[bSG[k]])
                    P.op("dve", (lambda f=f, sl=sl, b3=b3, k=k: lambda e: e.tensor_tensor(
                        out=G[:, f, sl], in0=BK[b3][:], in1=SG[k][:], op=ALU.mult))(),
                        reads=[bBK[b3], bSG[k]], writes=[bG[f][st]])
                    if extra:
                        extra.pop(0)()
        while extra:
            extra.pop(0)()
        for kb in range(8):
            W, bW = next_slab(f"{tag}.w2.{kb}")
            for st in range(NT):
                sl = slice(st * 512, (st + 1) * 512)
                by = 4 + (n % 2)
                n += 1
                for f in range(NF):
                    mm(BK[by][:], W[:, f * 128:(f + 1) * 128], G[:, f, sl], f == 0, f == NF - 1,
                       [bW, bG[f][st]], bBK[by], f == NF - 1)
                    if f == 10:
                        norm_flush_pending()
                P.op("dve", (lambda kb=kb, sl=sl, by=by: lambda e: e.scalar_tensor_tensor(
                    out=H[:, kb, sl], in0=BK[by][:], scalar=0.5, in1=H[:, kb, sl], op0=ALU.mult, op1=ALU.add))(),
                    reads=[bBK[by]], writes=[bH[kb][st]])
                norm_push(kb, st)


    MAGIC = 12582912.0
    INV2PI = float(np.float32(1.0 / (2.0 * np.pi)))
    C1 = 6.28125
    C2 = float(2.0 * np.pi - 6.28125)
    PI_LO = 3.1415925
    HPI = float(np.pi / 2)
    kvctr = {"kst": 0, "vst": 0, "stage": 0, "pt": 0, "sbank": 0}

    def trig_tables(i):
        t0 = i * T
        thunks = []

        class _Rec:
            def op(self, *a, **k):
                thunks.append(lambda: P.op(*a, **k))
        PR = _Rec()
        for q in range(NT):
            sl = slice(q * 512, (q + 1) * 512)
            PR.op("pool", (lambda q=q: lambda e: e.dma_start(
                out=POSI[:], in_=pos_d[0:1, t0 + q * 512:t0 + (q + 1) * 512].partition_broadcast(128)))(),
                writes=[bPOSI], dsem="posi")
            PR.op("dve", lambda e: e.tensor_scalar(out=TRG[0][:], in0=POSI[:], scalar1=col("invf"), scalar2=None,
                                                  op0=ALU.mult), reads=[bPOSI, bCOLS], writes=[bTRG[0]])
            PR.op("dve", lambda e: e.tensor_scalar(out=TRG[1][:], in0=TRG[0][:], scalar1=INV2PI, scalar2=MAGIC,
                                                  op0=ALU.mult, op1=ALU.add), reads=[bTRG[0]], writes=[bTRG[1]])
            PR.op("dve", lambda e: e.tensor_scalar(out=TRG[1][:], in0=TRG[1][:], scalar1=MAGIC, scalar2=None,
                                                  op0=ALU.subtract), reads=[bTRG[1]], writes=[bTRG[1]])
            for cc in (C1, C2):
                PR.op("dve", (lambda cc=cc: lambda e: e.scalar_tensor_tensor(
                    out=TRG[0][:], in0=TRG[1][:], scalar=-cc, in1=TRG[0][:], op0=ALU.mult, op1=ALU.add))(),
                    reads=[bTRG[1]], writes=[bTRG[0]])
            PR.op("dve", lambda e: e.tensor_scalar(out=TRG[0][:], in0=TRG[0][:], scalar1=-PI_LO, scalar2=PI_LO,
                                                  op0=ALU.max, op1=ALU.min), reads=[], writes=[bTRG[0]])
            PR.op("act", (lambda sl=sl: lambda e: e.activation(out=SIN[:, sl], in_=TRG[0][:], func=AF.Sin,
                                                              scale=col("sgn")))(),
                 reads=[bTRG[0], bCOLS], writes=[bSIN[q]])
            PR.op("dve", lambda e: e.tensor_scalar(out=TRG[2][:], in0=TRG[0][:], scalar1=HPI, scalar2=-2.0 * np.pi,
                                                  op0=ALU.is_gt, op1=ALU.mult), reads=[bTRG[0]], writes=[bTRG[2]])
            PR.op("dve", lambda e: e.scalar_tensor_tensor(out=TRG[1][:], in0=TRG[0][:], scalar=HPI, in1=TRG[2][:],
                                                         op0=ALU.add, op1=ALU.add),
                 reads=[bTRG[0], bTRG[2]], writes=[bTRG[1]])
            PR.op("dve", lambda e: e.tensor_scalar(out=TRG[1][:], in0=TRG[1][:], scalar1=-PI_LO, scalar2=PI_LO,
                                                  op0=ALU.max, op1=ALU.min), reads=[], writes=[bTRG[1]])
            PR.op("act", (lambda sl=sl: lambda e: e.activation(out=COS[:, sl], in_=TRG[1][:], func=AF.Sin))(),
                 reads=[bTRG[1]], writes=[bCOS[q]])
        return thunks

    def mla(i):
        t0 = i * T
        o_ = 0

        def carve(n):
            nonlocal o_
            a = AR[:, o_:o_ + n]
            o_ += n
            return a
        CQ = carve(4 * T).rearrange("p (c t) -> p c t", t=T)
        CKV = carve(2 * T).rearrange("p (c t) -> p c t", t=T)
        KST = [carve(T) for _ in range(4)]
        KRS = carve(T)
        VST = [carve(1024) for _ in range(4)]
        QN = [carve(T) for _ in range(2)]
        QR = [carve(T) for _ in range(2)]
        STK = [carve(T) for _ in range(2)]
        STR = [carve(T) for _ in range(2)]
        STV = [carve(T) for _ in range(2)]
        PT = [carve(512) for _ in range(4)]
        assert o_ <= ARW, o_
        leg = legacy_of(arena["bufs"])
        mk = lambda n: Buf(n, leg)
        bCQ = [[mk("cq") for s in range(NT)] for c in range(4)]
        bCKV = [[mk("ckv") for s in range(NT)] for c in range(2)]
        bKST = [mk(f"kst{k}") for k in range(4)]
        bKRS = [mk("krs") for s in range(NT)]
        bVST = [mk(f"vst{k}") for k in range(4)]
        bQN = [[mk("qn") for s in range(NT)] for _ in range(2)]
        bQR = [[mk("qr") for s in range(NT)] for _ in range(2)]
        bSTK = [mk("stk0"), mk("stk1")]
        bSTR = [mk("str0"), mk("str1")]
        bSTV = [mk("stv0"), mk("stv1")]
        bPT = [mk(f"pt{k}") for k in range(4)]
        arena["bufs"] = ([b for r_ in bCQ + bCKV + bQN + bQR for b in r_] + bKST + bKRS + bVST + bSTK + bSTR + bSTV
                         + bPT)

        def rope_combine(bank_x, bank_p, q, dst_ap, dst_buf):
            sl = slice(q * 512, (q + 1) * 512)
            P.op("dve", lambda e: e.tensor_tensor(out=TRG[0][:], in0=BK[bank_x][:], in1=COS[:, sl], op=ALU.mult),
                 reads=[bBK[bank_x], bCOS[q]], writes=[bTRG[0]])
            P.op("dve", lambda e: e.tensor_tensor(out=TRG[1][:], in0=BK[bank_p][:], in1=SIN[:, sl], op=ALU.mult),
                 reads=[bBK[bank_p], bSIN[q]], writes=[bTRG[1]])
            P.op("dve", lambda e: e.tensor_tensor(out=dst_ap, in0=TRG[0][:], in1=TRG[1][:], op=ALU.add),
                 reads=[bTRG[0], bTRG[1]], writes=[dst_buf])

        WA0, bWA0 = next_slab("wa.0")
        WA1, bWA1 = next_slab("wa.1", hold=1)
        for st in range(NT):
            sl = slice(st * 512, (st + 1) * 512)
            def a_block(blk):
                W, bW, ob = (WA0, bWA0, blk * 1024) if blk < 4 else (WA1, bWA1, (blk - 4) * 1024)
                for c in range(DC):
                    mm(BK[blk][:], W[:, ob + c * 128:ob + (c + 1) * 128], XN[:, c, sl], c == 0, c == DC - 1,
                       [bW, bXN[c][st]], bBK[blk], c == DC - 1)
            for blk in range(4):
                a_block(blk)
            for kk in range(2):
                ob = 2048 + kk * 1024
                for c in range(DC):
                    mm(BK[4 + kk][:], WA1[:, ob + c * 128:ob + (c + 1) * 128], XN[:, c, sl], c == 0, c == DC - 1,
                       [bWA1, bXN[c][st]], bBK[4 + kk], c == DC - 1)
            rope_combine(4, 5, st, KRS[:, sl], bKRS[st])
            sumsq_sbuf(lambda c: BK[c][:], [bBK[c] for c in range(4)], 4, 6, st)
            rstd_to(6, QL, RS[0][:], bRS[0])
            for c in range(4):
                P.op("dve", (lambda c=c, sl=sl: lambda e: e.scalar_tensor_tensor(
                    out=CQ[:, c, sl], in0=BK[c][:], scalar=col("qn", c), in1=RS[0][:], op0=ALU.mult, op1=ALU.mult))(),
                    reads=[bBK[c], bRS[0], bCOLS], writes=[bCQ[c][st]])
            for blk in range(4, 6):
                a_block(blk)
            sumsq_sbuf(lambda c: BK[4 + c][:], [bBK[4 + c] for c in range(2)], 2, 7, st)
            rstd_to(7, KVL, RS[1][:], bRS[1], li=1)
            for c in range(2):
                P.op("dve", (lambda c=c, sl=sl: lambda e: e.scalar_tensor_tensor(
                    out=CKV[:, c, sl], in0=BK[4 + c][:], scalar=col("kvn", c), in1=RS[1][:], op0=ALU.mult,
                    op1=ALU.mult))(),
                    reads=[bBK[4 + c], bRS[1], bCOLS], writes=[bCKV[c][st]])
        P.op("pool", lambda e: e.dma_start(out=krc_d[:, t0:t0 + T], in_=KRS), reads=bKRS, writes=[bRD[i]], dsem="krw")

        WKV, bWKV = next_slab("wukv")
        nb = 0
        for h in range(NH):
            ks = kvctr["kst"] % 4
            kvctr["kst"] += 1
            for st in range(NT):
                sl = slice(st * 512, (st + 1) * 512)
                bk = nb % 4
                nb += 1
                for c in range(2):
                    mm(BK[bk][:], WKV[:, (h * 2 + c) * 128:(h * 2 + c + 1) * 128], CKV[:, c, sl], c == 0, c == 1,
                       [bWKV, bCKV[c][st]], bBK[bk], c == 1)
                if nb % 2 == 0:
                    P.op("act", (lambda ks=ks, sl=sl, bk=bk: lambda e: e.activation(out=KST[ks][:, sl], in_=BK[bk][:],
                                                                                 func=AF.Copy))(),
                         reads=[bBK[bk]], writes=[bKST[ks]], partial=(st > 0))
                else:
                    P.op("dve", (lambda ks=ks, sl=sl, bk=bk: lambda e: e.tensor_copy(out=KST[ks][:, sl], in_=BK[bk][:]))(),
                         reads=[bBK[bk]], writes=[bKST[ks]], partial=(st > 0))
            P.op("pool", (lambda h=h, ks=ks: lambda e: e.dma_start(out=kc_d[:, h, t0:t0 + T], in_=KST[ks]))(),
                 reads=[bKST[ks]], writes=[bKD[i]], dsem=f"kw{ks}", partial=True)
        for tb in range(TB):
            vs = kvctr["vst"] % 4
            kvctr["vst"] += 1
            st = tb // 4
            for half in range(2):
                bk = nb % 4
                nb += 1
                for c in range(2):
                    mm(BK[bk][:], CKV[:, c, tb * 128:(tb + 1) * 128],
                       WKV[:, 2048 + c * 1024 + half * 512:2048 + c * 1024 + (half + 1) * 512], c == 0, c == 1,
                       [bWKV, bCKV[c][st]], bBK[bk], c == 1)
                if nb % 2 == 0:
                    P.op("act", (lambda vs=vs, half=half, bk=bk: lambda e: e.activation(
                        out=VST[vs][:, half * 512:(half + 1) * 512], in_=BK[bk][:], func=AF.Copy))(),
                        reads=[bBK[bk]], writes=[bVST[vs]], partial=(half > 0))
                else:
                    P.op("dve", (lambda vs=vs, half=half, bk=bk: lambda e: e.tensor_copy(
                        out=VST[vs][:, half * 512:(half + 1) * 512], in_=BK[bk][:]))(),
                        reads=[bBK[bk]], writes=[bVST[vs]], partial=(half > 0))
            P.op("pool", (lambda tb=tb, vs=vs: lambda e: e.dma_start(
                out=vc_d[:, :, i * TB + tb, :], in_=VST[vs].rearrange("p (h d) -> p h d", d=128)))(),
                reads=[bVST[vs]], writes=[bVD[i]], dsem=f"vw{vs}", partial=True)

        stages = [(h, j) for h in range(NH) for j in range(i + 1)]
        loaded = {"n": 0}

        def load_stage(idx):
            h, j = stages[idx]
            sidx = kvctr["stage"] + idx
            s = sidx % 2
            P.op("pool", lambda e: e.dma_start(out=STK[s], in_=kc_d[:, h, j * T:(j + 1) * T]),
                 reads=[bKD[j]], writes=[bSTK[s]], dsem=f"sk{s}")
            P.op("pool", lambda e: e.dma_start(out=STR[s], in_=krc_d[:, j * T:(j + 1) * T]),
                 reads=[bRD[j]], writes=[bSTR[s]], dsem=f"sr{s}")
            P.op("pool", lambda e: e.dma_start(out=STV[s].rearrange("p (b d) -> p b d", d=128),
                                               in_=vc_d[:, h, j * TB:(j + 1) * TB, :]),
                 reads=[bVD[j]], writes=[bSTV[s]], dsem=f"sv{s}")

        def ensure_stage(idx):
            while loaded["n"] <= min(idx, len(stages) - 1):
                load_stage(loaded["n"])
                loaded["n"] += 1
        cur = {"idx": -1, "cnt": 0}

        OB = [3, 4]
        DB = [5, 6]
        sidx0 = kvctr["stage"]
        wq = {}

        def q_part(hn, st, part):
            if hn % 2 == 0 and hn not in wq:
                wq[hn] = next_slab(f"wuq.{hn // 2}", hold=(1 if hn > 0 else 0))
                wq[hn + 1] = wq[hn]
            WQ, bWQ = wq[hn]
            hs_ = hn % 2
            sl = slice(st * 512, (st + 1) * 512)
            for c in range(4):
                ob = hs_ * 1536 + part * 512 + c * 128
                mm(BK[7][:], WQ[:, ob:ob + 128], CQ[:, c, sl], c == 0, c == 3, [bWQ, bCQ[c][st]], bBK[7], c == 3)
            if part == 0:
                P.op("act", lambda e: e.activation(out=QN[hs_][:, sl], in_=BK[7][:], func=AF.Copy),
                     reads=[bBK[7]], writes=[bQN[hs_][st]])
            elif part == 1:
                P.op("dve", lambda e: e.tensor_tensor(out=TRG[0][:], in0=BK[7][:], in1=COS[:, sl], op=ALU.mult),
                     reads=[bBK[7], bCOS[st]], writes=[bTRG[0]])
            else:
                P.op("dve", lambda e: e.tensor_tensor(out=TRG[1][:], in0=BK[7][:], in1=SIN[:, sl], op=ALU.mult),
                     reads=[bBK[7], bSIN[st]], writes=[bTRG[1]])
                P.op("dve", lambda e: e.tensor_tensor(out=QR[hs_][:, sl], in0=TRG[0][:], in1=TRG[1][:], op=ALU.add),
                     reads=[bTRG[0], bTRG[1]], writes=[bQR[hs_][st]])

        carry = []

        def flush_carry():
            while carry:
                st, sl, h_ = carry.pop(0)
                mm(BK[7][:], ONES[:], SQ[st % 2][:], True, True, [bONES, bSQ[st % 2]], bBK[7], True)
                P.op("act", (lambda st=st: lambda e: e.activation(out=LNT[st % 2][:], in_=BK[7][:], func=AF.Ln))(),
                     reads=[bBK[7]], writes=[bLNT[st % 2]])
                P.op("act", (lambda st=st: lambda e: e.activation(out=RS[st % 2][:], in_=LNT[st % 2][:], func=AF.Exp,
                                                                  scale=-1.0))(),
                     reads=[bLNT[st % 2]], writes=[bRS[st % 2]])
                P.op("dve", (lambda st=st, sl=sl, h_=h_: lambda e: e.tensor_tensor(
                    out=XN[:, h_, sl], in0=SG[st % 2][:], in1=RS[st % 2][:], op=ALU.mult))(),
                    reads=[bSG[st % 2], bRS[st % 2]], writes=[bXN[h_][st]])

        qparts = [(st, part) for st in range(NT) for part in range(3)]
        for st, part in qparts:
            q_part(0, st, part)
        for h in range(NH):
            hs = h % 2
            items = []
            for j in range(i + 1):
                for kb in range(TB):
                    for st in range(NT):
                        if j == i:
                            if kb > 4 * st + 3:
                                continue
                            off = max(0, (kb - 4 * st) * 128)
                            diag = kb >= 4 * st
                        else:
                            off, diag = 0, False
                        items.append((j, kb, st, off, diag))
            first = {st: True for st in range(NT)}
            lastidx = {}
            for n_, it in enumerate(items):
                lastidx[it[2]] = n_
            pend = []
            deferred = []

            def emit_pv(ent):
                n_, (j, kb, st, off, diag), s, pk = ent
                fl = first[st]
                first[st] = False
                la = lastidx[st] == n_
                mm(BK[OB[st]][:, off:512], STV[s][:, kb * 128:(kb + 1) * 128], PT[pk][:, off:512], fl, la,
                   [bSTV[s], bPT[pk]], bBK[OB[st]], True)
                if fl:
                    assert off == 0
                    P.op("dve", (lambda st=st, pk=pk: lambda e: e.tensor_copy(out=BK[DB[st]][:], in_=PT[pk][:]))(),
                         reads=[bPT[pk]], writes=[bBK[DB[st]]])
                else:
                    P.op("dve", (lambda st=st, pk=pk, off=off: lambda e: e.tensor_tensor(
                        out=BK[DB[st]][:, off:512], in0=BK[DB[st]][:, off:512], in1=PT[pk][:, off:512], op=ALU.add))(),
                        reads=[bPT[pk]], writes=[bBK[DB[st]]])
                if la:
                    sl = slice(st * 512, (st + 1) * 512)
                    P.op("act", (lambda st=st: lambda e: e.activation(out=SG[st % 2][:], in_=BK[OB[st]][:],
                                                                      func=AF.Copy))(),
                         reads=[bBK[OB[st]]], writes=[bSG[st % 2]])
                    P.op("dve", (lambda st=st: lambda e: e.tensor_copy(out=SQ[st % 2][:], in_=BK[DB[st]][:]))(),
                         reads=[bBK[DB[st]]], writes=[bSQ[st % 2]])
                    deferred.append((st, sl, h))

            nxt = list(qparts) if h + 1 < NH else []
            spacing = max(1, len(items) // 7)
            for n_, it in enumerate(items):
                j, kb, st, off, diag = it
                if nxt and n_ > 0 and n_ % spacing == 0:
                    q_part(h + 1, *nxt.pop(0))
                if n_ == 3:
                    flush_carry()
                idx = h * (i + 1) + j
                if idx != cur["idx"]:
                    cur["idx"] = idx
                    cur["cnt"] = 0
                    ensure_stage(idx)
                cur["cnt"] += 1
                s = (sidx0 + idx) % 2
                sb_ = kvctr["sbank"] % 3
                kvctr["sbank"] += 1
                pk = kvctr["pt"] % 4
                kvctr["pt"] += 1
                q0 = st * 512 + off
                q1 = (st + 1) * 512
                mm(BK[sb_][:, off:512], STK[s][:, kb * 128:(kb + 1) * 128], QN[hs][:, q0:q1], True, False,
                   [bSTK[s], bQN[hs][st]], bBK[sb_], False)
                mm(BK[sb_][:, off:512], STR[s][:, kb * 128:(kb + 1) * 128], QR[hs][:, q0:q1], False, True,
                   [bSTR[s], bQR[hs][st]], bBK[sb_], True)
                P.op("act", (lambda sb_=sb_, pk=pk, off=off: lambda e: e.activation(
                    out=PT[pk][:, off:512], in_=BK[sb_][:, off:512], func=AF.Exp, scale=SCALE))(),
                    reads=[bBK[sb_]], writes=[bPT[pk]])
                if diag:
                    P.op("dve", (lambda pk=pk, off=off: lambda e: e.memset(PT[pk][64:128, off:off + 64], 0.0))(),
                         writes=[bPT[pk]], partial=True)
                pend.append((n_, it, s, pk))
                if len(pend) > 2:
                    emit_pv(pend.pop(0))
                if cur["cnt"] == 3:
                    ensure_stage(idx + 1)
            while pend:
                emit_pv(pend.pop(0))
            carry.extend(deferred)
            if h == NH - 1:
                flush_carry()
            while nxt:
                q_part(h + 1, *nxt.pop(0))
        kvctr["stage"] += len(stages)
        if DEBUG:
            P.op("pool", lambda e: e.dma_start(out=dbg_d[:, :], in_=XN[:].rearrange("p c t -> p (c t)")),
                 reads=[b for r_ in bXN for b in r_], dsem="dbg")
            P.op("pool", lambda e: e.dma_start(out=dbg2_d[:, :], in_=AR[:, 0:6 * T]),
                 reads=[b for r_ in bCQ + bCKV for b in r_], dsem="dbg2")

        n = 0
        for s2 in range(2):
            W, bW = next_slab(f"wo.{s2}")
            for j in range(4):
                kb = 4 * s2 + j
                for st in range(NT):
                    sl = slice(st * 512, (st + 1) * 512)
                    by = n % 3
                    n += 1
                    for hh in range(NH):
                        mm(BK[by][:], W[:, (j * 8 + hh) * 128:(j * 8 + hh + 1) * 128], XN[:, hh, sl], hh == 0, hh == NH - 1,
                           [bW, bXN[hh][st]], bBK[by], hh == NH - 1)
                    P.op("dve", (lambda kb=kb, sl=sl, by=by: lambda e: e.tensor_tensor(
                        out=H[:, kb, sl], in0=BK[by][:], in1=H[:, kb, sl], op=ALU.add))(),
                        reads=[bBK[by]], writes=[bH[kb][st]])
                    norm_push(kb, st)

    def load_x_block(i, tb, slot):
        t0 = i * T + tb * 128
        P.op("pool", lambda e: e.dma_start(out=IOB[slot][:], in_=x_d[t0:t0 + 128, :]),
             writes=[bIOB[slot]], dsem=f"io{slot}")

    ioctr = {"n": 0}

    for i in range(NTILES):
        norm_begin(xn_target("fn1", 0))
        for tb in range(TB):
            slot = ioctr["n"] % 4
            ioctr["n"] += 1
            load_x_block(i, tb, slot)
            st = tb // 4
            for half in range(2):
                bk = rr["bank"] % 4
                rr["bank"] += 1
                for j in range(4):
                    c = half * 4 + j
                    P.op("pe", (lambda c=c, j=j, bk=bk, slot=slot: lambda e: e.transpose(
                        BK[bk][:, j * 128:(j + 1) * 128], IOB[slot][:, c * 128:(c + 1) * 128], IDENT[:]))(),
                        reads=[bIOB[slot], bIDENT], writes=[bBK[bk]], sig=(j == 3))
                P.op("act", (lambda half=half, tb=tb, bk=bk: lambda e: e.activation(
                    out=H[:, half * 4:half * 4 + 4, tb * 128:(tb + 1) * 128],
                    in_=BK[bk][:].rearrange("p (a b) -> p a b", b=128), func=AF.Copy))(),
                    reads=[bBK[bk]], writes=[bH[half * 4 + j][st] for j in range(4)], partial=True)
            if tb % 4 == 3:
                for c in range(DC):
                    norm_push(c, st)
        to_XN("fn1", 0)
        norm_begin(xn_target("mix", 0))
        ffn("L0f1", extra=trig_tables(i))
        to_XN("mix", 0)
        norm_begin(None)
        UW = HALO + T
        U = AR[:, 0:DC * UW].rearrange("p (c t) -> p c t", t=UW)
        cbase = DC * UW
        Cf = AR[:, cbase:cbase + 2 * DC * 512].bitcast(F32).rearrange("p (c t) -> p c t", t=512)
        leg = legacy_of(arena["bufs"])
        bUh = [Buf(f"Uh{c}", leg) for c in range(DC)]
        bU = [[Buf(f"U{c}.{s}", leg) for s in range(NT)] for c in range(DC)]
        bC = [Buf(f"C{c}", leg) for c in range(DC)]
        arena["bufs"] = bUh + [b for r_ in bU for b in r_] + bC
        for c in range(DC):
            P.op("dve", (lambda c=c: lambda e: e.tensor_copy(out=U[:, c, 0:HALO], in_=UH[:, c, :]))(),
                 reads=[bUH], writes=[bUh[c]])
        n = 0
        for s in range(4):
            W, bW = next_slab(f"pw1.{s}")
            for j in range(2):
                ci = 2 * s + j
                for st in range(NT):
                    sl = slice(st * 512, (st + 1) * 512)
                    ba, bb = (n % 2), 2 + (n % 2)
                    n += 1
                    for c in range(DC):
                        o_ = (j * 16 + c) * 128
                        mm(BK[ba][:], W[:, o_:o_ + 128], XN[:, c, sl], c == 0, c == DC - 1, [bW, bXN[c][st]], bBK[ba],
                           c == DC - 1)
                    for c in range(DC):
                        o_ = (j * 16 + 8 + c) * 128
                        mm(BK[bb][:], W[:, o_:o_ + 128], XN[:, c, sl], c == 0, c == DC - 1, [bW, bXN[c][st]], bBK[bb],
                           c == DC - 1)
                    k = rr["sg"] % 2
                    rr["sg"] += 1
                    P.op("act", (lambda bb=bb, k=k: lambda e: e.activation(out=SG[k][:], in_=BK[bb][:], func=AF.Sigmoid))(),
                         reads=[bBK[bb]], writes=[bSG[k]])
                    P.op("dve", (lambda ci=ci, st=st, ba=ba, k=k: lambda e: e.tensor_tensor(
                        out=U[:, ci, HALO + st * 512:HALO + (st + 1) * 512], in0=BK[ba][:], in1=SG[k][:], op=ALU.mult))(),
                        reads=[bBK[ba], bSG[k]], writes=[bU[ci][st]])
        CfS = [Cf]
        bCS = [bC]
        if NT > 1:
            c2 = cbase + 2 * DC * 512
            assert c2 + 2 * DC * 512 <= ARW, "arena too small for conv"
            CfS.append(AR[:, c2:c2 + 2 * DC * 512].bitcast(F32).rearrange("p (c t) -> p c t", t=512))
            bC2 = [Buf(f"C2{c}", leg) for c in range(DC)]
            bCS.append(bC2)
            arena["bufs"] += bC2
        for ci in range(DC):
            W, bW = next_slab(f"dw.{ci}")
            for st in range(NT):
                bc = 4 + (n % 2)
                n += 1
                rd = [bW, bU[ci][st]] + ([bUh[ci]] if st == 0 else [bU[ci][st - 1]])
                for j in range(CW):
                    c0 = st * 512 + 2 + j
                    mm(BK[bc][:], W[:, j * 128:(j + 1) * 128], U[:, ci, c0:c0 + 512], j == 0, j == CW - 1, rd, bBK[bc],
                       j == CW - 1)
                P.op("act", (lambda ci=ci, st=st, bc=bc: lambda e: e.activation(
                    out=CfS[st][:, ci, :], in_=BK[bc][:], func=AF.Copy))(),
                    reads=[bBK[bc]], writes=[bCS[st][ci]])
                norm_push_src(CfS[st][:, ci, :], bCS[st][ci], st)
        for c in range(DC):
            P.op("dve", (lambda c=c: lambda e: e.tensor_copy(out=UH[:, c, :], in_=U[:, c, T:T + HALO]))(),
                 reads=[bU[c][NT - 1]], writes=[bUH], partial=True)
        norm_stats_finish()
        norm_begin(xn_target("fn2", 0))
        for st in range(NT):
            sl = slice(st * 512, (st + 1) * 512)
            for c in range(DC):
                P.op("dve", (lambda c=c, st=st: lambda e: e.scalar_tensor_tensor(
                    out=CfS[st][:, c, :], in0=CfS[st][:, c, :], scalar=col("cnorm", c), in1=BK[7 - st][:],
                    op0=ALU.mult, op1=ALU.mult))(),
                    reads=[bBK[7 - st], bCOLS], writes=[bCS[st][c]])
                P.op("act", (lambda c=c, st=st, sl=sl: lambda e: e.activation(
                    out=XN[:, c, sl], in_=CfS[st][:, c, :], func=AF.Silu))(),
                    reads=[bCS[st][c]], writes=[bXN[c][st]])
        for s in range(2):
            W, bW = next_slab(f"pw2.{s}")
            for j in range(4):
                kb = 4 * s + j
                for st in range(NT):
                    sl = slice(st * 512, (st + 1) * 512)
                    by = 4 + (n % 2)
                    n += 1
                    for c in range(DC):
                        mm(BK[by][:], W[:, (j * 8 + c) * 128:(j * 8 + c + 1) * 128], XN[:, c, sl], c == 0, c == DC - 1,
                           [bW, bXN[c][st]], bBK[by], c == DC - 1)
                    P.op("dve", (lambda kb=kb, sl=sl, by=by: lambda e: e.tensor_tensor(
                        out=H[:, kb, sl], in0=BK[by][:], in1=H[:, kb, sl], op=ALU.add))(),
                        reads=[bBK[by]], writes=[bH[kb][st]])
                    norm_push(kb, st)
        to_XN("fn2", 0)
        norm_begin(xn_target("fn1", 8))
        ffn("L0f2")
        to_XN("fn1", 8)
        norm_begin(xn_target("mix", 8))
        ffn("L1f1")
        to_XN("mix", 8)
        norm_begin(xn_target("fn2", 8))
        mla(i)
        to_XN("fn2", 8)
        norm_begin(("final", 0, lambda c, sl: H[:, c, sl], lambda c, st: bH[c][st], True))
        ffn("L1f2")
        norm_finish_deferred()
        for tb in range(TB):
            st = tb // 4
            slot = ioctr["n"] % 4
            ioctr["n"] += 1
            for half in range(2):
                bk = rr["bank"] % 4
                rr["bank"] += 1
                for j in range(4):
                    c = half * 4 + j
                    P.op("pe", (lambda c=c, j=j, bk=bk, tb=tb: lambda e: e.transpose(
                        BK[bk][:, j * 128:(j + 1) * 128], H[:, c, tb * 128:(tb + 1) * 128], IDENT[:]))(),
                        reads=[bH[c][st], bIDENT], writes=[bBK[bk]], sig=(j == 3))
                    pe_tick()
                P.op("act", (lambda half=half, bk=bk, slot=slot: lambda e: e.activation(
                    out=IOB[slot][:, half * 512:(half + 1) * 512], in_=BK[bk][:], func=AF.Copy))(),
                    reads=[bBK[bk]], writes=[bIOB[slot]], partial=(half == 1))
            t0 = i * T + tb * 128
            P.op("pool", (lambda t0=t0, slot=slot: lambda e: e.dma_start(out=out_d[t0:t0 + 128, :], in_=IOB[slot][:]))(),
                 reads=[bIOB[slot]], dsem=f"io{slot}")

    sems = {}
    for k in sorted(P.semkeys):
        sems[k] = es.enter_context(nc.semaphore(k))
    final_waits = [(k, P.semcnt[k]) for k in ("io0", "io1", "io2", "io3")]
    with nc.Block() as block:
        def replay(E, eng, extra=()):
            for waits, fn, inc in E.ops:
                for k, v in waits:
                    eng.wait_ge(sems[k], v)
                ins = fn(eng)
                if inc is not None:
                    ins.then_inc(sems[inc[0]], inc[1])
            for k, v in extra:
                eng.wait_ge(sems[k], v)

        @block.tensor
        def _(e):
            replay(P.q["pe"], e)

        @block.scalar
        def _(e):
            replay(P.q["act"], e)

        @block.vector
        def _(e):
            replay(P.q["dve"], e)

        @block.sync
        def _(e):
            replay(P.q["sp"], e)

        @block.gpsimd
        def _(e):
            replay(P.q["pool"], e, extra=final_waits)
    es.close()
    return nc


_CACHE = {}


def _get_program(SEQ):
    if SEQ not in _CACHE:
        _CACHE[SEQ] = build_program(SEQ)
    return _CACHE[SEQ]


def kernel(**inputs):
    p = {k: np.asarray(v) for k, v in inputs.items()}
    x = p["x"]
    B, SEQ, _ = x.shape
    assert B == 8 and SEQ % T == 0
    nc = _get_program(SEQ)
    wall = pack_weights(p)
    cols = pack_cols(p)
    ident = np.eye(128, dtype=np.float32)
    pos = p["positions"].astype(np.int32)
    in_maps = [{"x": np.ascontiguousarray(x[b]), "pos": np.ascontiguousarray(pos[b:b + 1]), "wall": wall,
                "cols": cols, "ident": ident} for b in range(B)]
    res = run_bass_kernel_spmd(nc, in_maps, core_ids=list(range(B)))
    return np.stack([np.asarray(r["out"]) for r in res.results], axis=0).astype(np.float32)
```
